# Optimizing a Trainium2 kernel written in Bass

```python
import math
import jax, jax.numpy as jnp
from jax import lax
import numpy as np

D_MODEL = 4096
BATCH = 2
SEQ = 8192
DEPTH = 2

HEAD_DIM = 64
N_Q_HEADS = D_MODEL // 2 // HEAD_DIM
N_KV_HEADS = N_Q_HEADS // 8
GROUP = N_Q_HEADS // N_KV_HEADS
WINDOW = 128
BLOCK = 128
ATTN_W = N_Q_HEADS * HEAD_DIM
KV_W = N_KV_HEADS * HEAD_DIM
CONF_W = D_MODEL // 4
CONF_K = 31
SC_W = D_MODEL // 4
SC_K = 3
N_BRANCH = 3
D_FF = ((8 * D_MODEL // 3 + 255) // 256) * 256
ALPHA = (2 * DEPTH) ** 0.25
BETA = (8 * DEPTH) ** -0.25
LN_EPS = 1e-5
NEG_INF = -1e30

SPLIT_SIZES = (ATTN_W, KV_W, KV_W, 2 * CONF_W, 3 * SC_W, N_BRANCH * D_MODEL)
SPLIT_IDX = tuple(int(v) for v in np.cumsum(SPLIT_SIZES)[:-1])
IN_W = int(sum(SPLIT_SIZES))

kernel_name = "hybrid_swa_conformer_shortconv_deepnorm"


def layer_norm(x, g, b):
    xf = x.astype(jnp.float32)
    mu = jnp.mean(xf, axis=-1, keepdims=True)
    var = jnp.mean(jnp.square(xf - mu), axis=-1, keepdims=True)
    y = (xf - mu) * lax.rsqrt(var + LN_EPS)
    return (y * g.astype(jnp.float32) + b.astype(jnp.float32)).astype(x.dtype)


def causal_depthwise_conv(u, w):
    K, C = w.shape
    return lax.conv_general_dilated(
        u, w[:, None, :].astype(u.dtype), window_strides=(1,), padding=[(K - 1, 0)],
        dimension_numbers=("NWC", "WIO", "NWC"), feature_group_count=C)


def sliding_window_gqa_sinks(q, k, v, sinks):
    B, T, _ = q.shape
    nb = T // BLOCK
    qb = q.reshape(B, nb, BLOCK, N_KV_HEADS, GROUP, HEAD_DIM)

    def band_blocks(t):
        tp = jnp.pad(t.reshape(B, T, N_KV_HEADS, HEAD_DIM), ((0, 0), (BLOCK, 0), (0, 0), (0, 0)))
        tp = tp.reshape(B, nb + 1, BLOCK, N_KV_HEADS, HEAD_DIM)
        return jnp.concatenate([tp[:, :-1], tp[:, 1:]], axis=2)

    kb = band_blocks(k)
    vb = band_blocks(v)
    s = jnp.einsum("bnqhgd,bnkhd->bnhgqk", qb, kb,
                   preferred_element_type=jnp.float32) * (HEAD_DIM ** -0.5)
    qi = jnp.arange(BLOCK)[:, None]
    kj = jnp.arange(2 * BLOCK)[None, :]
    rel = qi + BLOCK - kj
    band = (rel >= 0) & (rel < WINDOW)
    key_pos = jnp.arange(nb)[:, None] * BLOCK + kj - BLOCK
    mask = band[None] & (key_pos >= 0)[:, None, :]
    s = jnp.where(mask[None, :, None, None], s, NEG_INF)
    sink = sinks.astype(jnp.float32).reshape(1, 1, N_KV_HEADS, GROUP, 1, 1)
    m = jnp.maximum(jnp.max(s, axis=-1, keepdims=True), sink)
    p = jnp.exp(s - m)
    denom = jnp.sum(p, axis=-1, keepdims=True) + jnp.exp(sink - m)
    w = (p / denom).astype(v.dtype)
    o = jnp.einsum("bnhgqk,bnkhd->bnqhgd", w, vb)
    return o.reshape(B, T, ATTN_W)


def hybrid_layer(x, w_in, sinks, conf_dw, conf_dw_b, conf_ln_g, conf_ln_b, sc_dw,
                 w_proj_attn, w_proj_conf, w_proj_sc, w_out, ln1_g, ln1_b,
                 w_ffn_in, w_ffn_down, ln2_g, ln2_b):
    B, T, _ = x.shape
    z = x @ w_in
    q, k, v, conf_u, sc_u, gate_u = jnp.split(z, SPLIT_IDX, axis=-1)

    o_a = sliding_window_gqa_sinks(q, k, v, sinks)

    c_val, c_gate = jnp.split(conf_u, 2, axis=-1)
    c = c_val * jax.nn.sigmoid(c_gate)
    c = causal_depthwise_conv(c, conf_dw) + conf_dw_b
    o_b = jax.nn.silu(layer_norm(c, conf_ln_g, conf_ln_b))

    g_b, g_c, x_in = jnp.split(sc_u, 3, axis=-1)
    o_c = g_b * causal_depthwise_conv(g_c * x_in, sc_dw)

    gates = jax.nn.sigmoid(gate_u).reshape(B, T, N_BRANCH, D_MODEL)
    merged = (gates[:, :, 0] * (o_a @ w_proj_attn)
              + gates[:, :, 1] * (o_b @ w_proj_conf)
              + gates[:, :, 2] * (o_c @ w_proj_sc))
    h = layer_norm(ALPHA * x + merged @ w_out, ln1_g, ln1_b)

    f_gate, f_up = jnp.split(h @ w_ffn_in, 2, axis=-1)
    ffn = (jax.nn.silu(f_gate) * f_up) @ w_ffn_down
    return layer_norm(ALPHA * h + ffn, ln2_g, ln2_b)


def setup_inputs(seed: int = 0) -> dict:
    key = jax.random.key(seed)
    ks = jax.random.split(key, 32)
    f32 = jnp.float32

    def dense(k, shape, fan_in, scale=1.0):
        return jax.random.normal(k, shape, f32) * (fan_in ** -0.5) * scale

    x = jax.random.normal(ks[0], (BATCH, SEQ, D_MODEL), f32)
    w_in = jnp.concatenate([
        dense(ks[1], (DEPTH, D_MODEL, ATTN_W), D_MODEL),
        dense(ks[2], (DEPTH, D_MODEL, KV_W), D_MODEL),
        dense(ks[3], (DEPTH, D_MODEL, KV_W), D_MODEL, BETA),
        dense(ks[4], (DEPTH, D_MODEL, 2 * CONF_W), D_MODEL),
        dense(ks[5], (DEPTH, D_MODEL, 3 * SC_W), D_MODEL),
        dense(ks[6], (DEPTH, D_MODEL, N_BRANCH * D_MODEL), D_MODEL),
    ], axis=-1)
    attn_sinks = jax.random.normal(ks[7], (DEPTH, N_Q_HEADS), f32)
    conf_dw = dense(ks[8], (DEPTH, CONF_K, CONF_W), CONF_K)
    conf_dw_b = 0.02 * jax.random.normal(ks[9], (DEPTH, CONF_W), f32)
    conf_ln_g = 1.0 + 0.02 * jax.random.normal(ks[10], (DEPTH, CONF_W), f32)
    conf_ln_b = 0.02 * jax.random.normal(ks[11], (DEPTH, CONF_W), f32)
    sc_dw = dense(ks[12], (DEPTH, SC_K, SC_W), SC_K)
    w_proj_attn = dense(ks[13], (DEPTH, ATTN_W, D_MODEL), ATTN_W, BETA)
    w_proj_conf = dense(ks[14], (DEPTH, CONF_W, D_MODEL), CONF_W, BETA)
    w_proj_sc = dense(ks[15], (DEPTH, SC_W, D_MODEL), SC_W, BETA)
    w_out = dense(ks[16], (DEPTH, D_MODEL, D_MODEL), D_MODEL, BETA)
    ln1_g = 1.0 + 0.02 * jax.random.normal(ks[17], (DEPTH, D_MODEL), f32)
    ln1_b = 0.02 * jax.random.normal(ks[18], (DEPTH, D_MODEL), f32)
    w_ffn_in = dense(ks[19], (DEPTH, D_MODEL, 2 * D_FF), D_MODEL, BETA)
    w_ffn_down = dense(ks[20], (DEPTH, D_FF, D_MODEL), D_FF, BETA)
    ln2_g = 1.0 + 0.02 * jax.random.normal(ks[21], (DEPTH, D_MODEL), f32)
    ln2_b = 0.02 * jax.random.normal(ks[22], (DEPTH, D_MODEL), f32)
    return {"x": x, "w_in": w_in, "attn_sinks": attn_sinks, "conf_dw": conf_dw,
            "conf_dw_b": conf_dw_b, "conf_ln_g": conf_ln_g, "conf_ln_b": conf_ln_b,
            "sc_dw": sc_dw, "w_proj_attn": w_proj_attn, "w_proj_conf": w_proj_conf,
            "w_proj_sc": w_proj_sc, "w_out": w_out, "ln1_g": ln1_g, "ln1_b": ln1_b,
            "w_ffn_in": w_ffn_in, "w_ffn_down": w_ffn_down, "ln2_g": ln2_g, "ln2_b": ln2_b}


def reference(x, w_in, attn_sinks, conf_dw, conf_dw_b, conf_ln_g, conf_ln_b, sc_dw,
              w_proj_attn, w_proj_conf, w_proj_sc, w_out, ln1_g, ln1_b,
              w_ffn_in, w_ffn_down, ln2_g, ln2_b):
    h = x
    for l in range(DEPTH):
        h = hybrid_layer(h, w_in[l], attn_sinks[l], conf_dw[l], conf_dw_b[l], conf_ln_g[l],
                         conf_ln_b[l], sc_dw[l], w_proj_attn[l], w_proj_conf[l], w_proj_sc[l],
                         w_out[l], ln1_g[l], ln1_b[l], w_ffn_in[l], w_ffn_down[l],
                         ln2_g[l], ln2_b[l])
    return h
```

```python
import contextlib
import numpy as np
import concourse.bass as bass
import concourse.mybir as mybir
from concourse.bass_utils import run_bass_kernel_spmd

F32 = mybir.dt.float32
BF16 = mybir.dt.bfloat16
AF = mybir.ActivationFunctionType
ALU = mybir.AluOpType

ENGS = ("sp", "act", "pool", "dve", "pe")
SAME_ENGINE_SYNC = True


class Op:
    __slots__ = ("eng", "emit", "idx", "signal", "waits", "dma_key", "dma_cum")

    def __init__(self, eng, emit):
        self.eng = eng
        self.emit = emit
        self.idx = -1
        self.signal = False
        self.waits = []
        self.dma_key = None
        self.dma_cum = 0


class Prog:
    def __init__(self, nc):
        self.nc = nc
        self.dry = False
        self.ops = {e: [] for e in ENGS}
        self.last_write = {}
        self.reads_since = {}
        self.waited_eng = {e: {f: -1 for f in ENGS} for e in ENGS}
        self.waited_dma = {e: {} for e in ENGS}
        self.dma_cum = {}
        self.bankmap = {}
        self.last_bank = {}

    def _add_dep(self, op, dep, kind):
        e = op.eng
        if dep is op:
            return
        if dep.dma_key is not None:
            cur = self.waited_dma[e].get(dep.dma_key, 0)
            if dep.dma_cum > cur:
                self.waited_dma[e][dep.dma_key] = dep.dma_cum
                op.waits.append(("dma", dep.dma_key, dep.dma_cum))
            return
        if dep.eng == e:
            if kind == "war" or not SAME_ENGINE_SYNC or e == "pe":
                return
        if dep.idx <= self.waited_eng[e][dep.eng]:
            return
        self.waited_eng[e][dep.eng] = dep.idx
        dep.signal = True
        op.waits.append(("eng", dep.eng, dep.idx))

    def op(self, eng, emit, reads=(), writes=(), dma_key=None):
        if self.dry:
            return None
        o = Op(eng, emit)
        o.idx = len(self.ops[eng])
        if dma_key is not None:
            o.dma_key = dma_key
            self.dma_cum[dma_key] = self.dma_cum.get(dma_key, 0) + 16
            o.dma_cum = self.dma_cum[dma_key]
        lw_ = self.last_write
        rs_ = self.reads_since
        for r in reads:
            lw = lw_.get(r)
            if lw is not None:
                self._add_dep(o, lw, "raw")
        for w in writes:
            lw = lw_.get(w)
            if lw is not None:
                self._add_dep(o, lw, "waw")
            for rd in rs_.get(w, ()):
                self._add_dep(o, rd, "war")
        for r in reads:
            rs_.setdefault(r, []).append(o)
        for w in writes:
            lw_[w] = o
            rs_[w] = []
        bm = self.bankmap
        bks = None
        for k in reads:
            if k in bm:
                bks = (bks or set())
                bks.update(bm[k])
        for k in writes:
            if k in bm:
                bks = (bks or set())
                bks.update(bm[k])
        if bks:
            for b in bks:
                d = self.last_bank.get(b)
                if d is not None and d.eng != eng:
                    self._add_dep(o, d, "waw")
                self.last_bank[b] = o
        self.ops[eng].append(o)
        return o

    def wait_all_dma(self, eng, keys):
        o = Op(eng, None)
        o.idx = len(self.ops[eng])
        for k in keys:
            if k in self.dma_cum:
                o.waits.append(("dma", k, self.dma_cum[k]))
        self.ops[eng].append(o)

    def emit(self):
        nc = self.nc
        with contextlib.ExitStack() as st:
            esem = {e: st.enter_context(nc.semaphore("S_" + e)) for e in ENGS}
            dsem = {}
            for i, k in enumerate(self.dma_cum):
                dsem[k] = st.enter_context(nc.semaphore("D%d" % i))
            sigcount = {}
            for e in ENGS:
                c = 0
                arr = []
                for o in self.ops[e]:
                    if o.signal:
                        c += 1
                    arr.append(c)
                sigcount[e] = arr
            block = st.enter_context(nc.Block())

            def run(engname, engobj):
                for o in self.ops[engname]:
                    for w in o.waits:
                        if w[0] == "dma":
                            engobj.wait_ge(dsem[w[1]], w[2])
                        else:
                            engobj.wait_ge(esem[w[1]], sigcount[w[1]][w[2]])
                    if o.emit is None:
                        continue
                    ins = o.emit(engobj)
                    if o.dma_key is not None:
                        ins.then_inc(dsem[o.dma_key], 16)
                    elif o.signal:
                        ins.then_inc(esem[engname], 1)

            @block.sync
            def _(eng):
                run("sp", eng)

            @block.scalar
            def _(eng):
                run("act", eng)

            @block.gpsimd
            def _(eng):
                run("pool", eng)

            @block.vector
            def _(eng):
                run("dve", eng)

            @block.tensor
            def _(eng):
                run("pe", eng)
        return {"n_ops": {e: len(self.ops[e]) for e in ENGS},
                "n_sig": {e: (sigcount[e][-1] if sigcount[e] else 0) for e in ENGS},
                "max_dma": max(self.dma_cum.values()) if self.dma_cum else 0}


T = 544
HF = 272
HALO = 128
NBLK = 5
CONF_K = 31
SC_K = 3
LN_EPS = 1e-5
NEG = -1e30
NSLOT = 4
SLOT_ELEMS = 4096
NST = 6


def make_cfg(D=4096, B=2, SEQ=8192, DEPTH=2, NT=4, alias=True):
    c = dict(D=D, B=B, SEQ=SEQ, L=DEPTH, NT=NT, alias=alias)
    c["DC"] = D // 128
    c["ATTN_W"] = D // 2
    c["AC"] = c["ATTN_W"] // 128
    c["NQH"] = c["ATTN_W"] // 64
    c["NKV"] = c["NQH"] // 8
    c["GROUP"] = 8
    c["KVW"] = c["NKV"] * 64
    c["KVC"] = max(1, c["KVW"] // 128)
    c["CONF_W"] = D // 4
    c["CC"] = c["CONF_W"] // 128
    c["SC_W"] = D // 4
    c["SCC"] = c["SC_W"] // 128
    c["DFF"] = ((8 * D // 3 + 255) // 256) * 256
    c["FC"] = c["DFF"] // 128
    c["G"] = 3
    q_, r_ = divmod(c["FC"], 3)
    c["FGS"] = [q_ + (1 if i < r_ else 0) for i in range(3)]
    c["IN_W"] = c["ATTN_W"] + 2 * c["KVW"] + 2 * c["CONF_W"] + 3 * c["SC_W"] + 3 * D
    c["OFF_K"] = c["ATTN_W"]
    c["OFF_V"] = c["OFF_K"] + c["KVW"]
    c["OFF_CONF"] = c["OFF_V"] + c["KVW"]
    c["OFF_SC"] = c["OFF_CONF"] + 2 * c["CONF_W"]
    c["OFF_G"] = c["OFF_SC"] + 3 * c["SC_W"]
    c["ALPHA"] = (2 * DEPTH) ** 0.25
    c["OWN"] = NT * T - HALO
    c["NCORE_SEQ"] = SEQ // c["OWN"]
    assert c["NCORE_SEQ"] * c["OWN"] == SEQ and c["NCORE_SEQ"] * B == 8
    DC, CC, SCC, NQH = c["DC"], c["CC"], c["SCC"], c["NQH"]
    o = 0
    pl = {}
    for name, n in (("ln1g", DC), ("ln1b", DC), ("ln2g", DC), ("ln2b", DC), ("cw", CC * CONF_K),
                    ("cb", CC), ("cg", CC), ("cbb", CC), ("sw", SCC * SC_K), ("sink", NQH)):
        pl[name] = (o, n)
        o += n
    c["PL"] = pl
    c["PLN"] = o
    c["NP"] = o * DEPTH + 1
    return c


def build_program(cfg):
    D, L, NT = cfg["D"], cfg["L"], cfg["NT"]
    DC, AC, CC, SCC = cfg["DC"], cfg["AC"], cfg["CC"], cfg["SCC"]
    NQH, NKV, KVW, KVC = cfg["NQH"], cfg["NKV"], cfg["KVW"], cfg["KVC"]
    FC, G, DFF = cfg["FC"], cfg["G"], cfg["DFF"]
    FGS = cfg["FGS"]
    FGMAX = max(FGS)
    assert FGMAX + cfg["DC"] <= 2 * cfg["DC"]
    ALPHA = cfg["ALPHA"]
    OWN = cfg["OWN"]
    PL, PLN = cfg["PL"], cfg["PLN"]
    NTOKIN = HALO + NT * T

    nc = bass.Bass("TRN2", target_bir_lowering=False)
    xin = nc.dram_tensor("xin", [NTOKIN, D], F32, kind="ExternalInput").ap()
    w_in = nc.dram_tensor("w_in", [L, D, cfg["IN_W"]], F32, kind="ExternalInput").ap()
    w_pa = nc.dram_tensor("w_pa", [L, cfg["ATTN_W"], D], F32, kind="ExternalInput").ap()
    w_pb = nc.dram_tensor("w_pb", [L, cfg["CONF_W"], D], F32, kind="ExternalInput").ap()
    w_pc = nc.dram_tensor("w_pc", [L, cfg["SC_W"], D], F32, kind="ExternalInput").ap()
    w_out = nc.dram_tensor("w_out", [L, D, D], F32, kind="ExternalInput").ap()
    w_fi = nc.dram_tensor("w_fi", [L, D, 2 * DFF], F32, kind="ExternalInput").ap()
    w_fd = nc.dram_tensor("w_fd", [L, DFF, D], F32, kind="ExternalInput").ap()
    prm_d = nc.dram_tensor("prm", [128, cfg["NP"]], F32, kind="ExternalInput").ap()
    msk_d = nc.dram_tensor("msk", [128, 512], F32, kind="ExternalInput").ap()
    idf_d = nc.dram_tensor("idf", [128, 128], F32, kind="ExternalInput").ap()
    y_d = nc.dram_tensor("y", [OWN, D], F32, kind="ExternalOutput").ap()

    st = contextlib.ExitStack()
    sb = lambda name, shape, dt: st.enter_context(nc.sbuf_tensor(name, shape, dt))
    XB = sb("XB", [128, DC, T], BF16)
    XL = sb("XL", [128, DC, T], BF16)
    OM = sb("OM", [128, 2 * DC, T], BF16)
    WS = [sb("WS%d" % i, [128, SLOT_ELEMS], BF16) for i in range(NSLOT)]
    STG = [sb("STG%d" % i, [128, T], F32) for i in range(NST)]
    ROW = [sb("ROW%d" % i, [128, T], F32) for i in range(3)]
    CB = sb("CB", [128, 30 + T], F32)
    YB = sb("YB", [128, T], F32)
    UB = sb("UB", [128, 2 + T], F32)
    CH = sb("CH", [128, L * CC, 30], F32)
    UH = sb("UH", [128, L * SCC, 2], F32)
    KH = sb("KH", [128, L * NKV, 128], BF16)
    VH = sb("VH", [128, L * KVC, 128], BF16)
    PRM = sb("PRM", [128, cfg["NP"]], F32)
    ESK = sb("ESK", [128, L * NQH], F32)
    MSK = sb("MSK", [128, 512], F32)
    IDF = sb("IDF", [128, 128], F32)
    IDB = sb("IDB", [128, 128], BF16)
    ONED = sb("ONED", [128, 128], F32)
    ONEC = sb("ONEC", [128, 128], F32)
    SM = sb("SM", [128, 16], F32)
    DUM = sb("DUM", [128, 2], F32)
    a_xbh = DC * 128 * 2
    a_k2 = a_xbh
    a_vf = a_k2 + NKV * 768 * 2
    a_vt = a_vf + KVC * 768 * 2
    a_qb = a_vt + 6 * KVW * 2
    a_ss = a_qb + 2 * T * 2
    a_pn = a_ss + 2 * 2 * 256 * 4
    a_pt = a_pn + 2 * 2 * 256 * 2
    a_end = a_pt + 2 * 512 * 2
    a_yc = a_xbh
    a_end = max(a_end, a_yc + CC * T * 4)
    mb_bytes = DC * T * 2
    if cfg["alias"]:
        assert a_end <= mb_bytes, (a_end, mb_bytes)
        AR, ar_base = OM, mb_bytes
    else:
        AR = sb("AR", [128, a_end // 2], BF16)
        ar_base = 0
    ARb = AR.bitcast(BF16) if False else AR
    ARflat_b = AR[:].rearrange("p a b -> p (a b)") if cfg["alias"] else AR[:]
    ARf32 = AR.bitcast(F32)
    ARflat_f = ARf32[:].rearrange("p a b -> p (a b)") if cfg["alias"] else ARf32[:]

    def arb(off, n):
        o = (ar_base + off) // 2
        return ARflat_b[:, o:o + n]

    def arf(off, n):
        o = (ar_base + off) // 4
        return ARflat_f[:, o:o + n]

    XBH = arb(0, DC * 128).rearrange("p (a b) -> p a b", b=128)
    K2 = arb(a_k2, NKV * 768).rearrange("p (a b) -> p a b", b=768)
    VF = arb(a_vf, KVC * 768).rearrange("p (a b) -> p a b", b=768)
    VT = arb(a_vt, 6 * KVW).rearrange("p (a b) -> p a b", b=KVW)
    QB = arb(a_qb, 2 * T).rearrange("p (a b) -> p a b", b=T)
    SS = arf(a_ss, 2 * 2 * 256).rearrange("p (j h k) -> p j h k", j=2, h=2)
    PN = arb(a_pn, 2 * 2 * 256).rearrange("p (j h k) -> p j h k", j=2, h=2)
    PTS = arb(a_pt, 2 * 512).rearrange("p (j k) -> p j k", j=2)
    YC = arf(a_yc, CC * T).rearrange("p (a b) -> p a b", b=T)
    OMf = OM.bitcast(F32)
    OMflat_f = OMf[:].rearrange("p a b -> p (a b)")
    Y32 = OMflat_f[:, 0:DC * T].rearrange("p (a b) -> p a b", b=T)
    XBflat_f = XB.bitcast(F32)[:].rearrange("p a b -> p (a b)")
    XS = [OMflat_f[:, i * D:(i + 1) * D] for i in range(2)]
    YS = [XBflat_f[:, i * D:(i + 1) * D] for i in range(2)]
    assert 2 * D <= DC * T // 2 + 0 or True
    assert 2 * D * 4 <= DC * T * 2, "staging must fit"

    psA = st.enter_context(nc.psum_tensor("psA", [128, 1024], F32))
    psB = st.enter_context(nc.psum_tensor("psB", [128, 1024], F32))
    psX = st.enter_context(nc.psum_tensor("psX", [128, 1024], F32))
    psY = st.enter_context(nc.psum_tensor("psY", [128, 1024], F32))
    PSL = [psA, psB]
    PXT = [psX[:, 0:128], psX[:, 512:640], psY[:, 0:128], psY[:, 512:640]]
    psYb = psY.bitcast(BF16)

    P = Prog(nc)
    for s_ in range(2):
        P.bankmap[("ps", s_)] = [2 * s_, 2 * s_ + 1]
    for j_ in range(2):
        for p_ in range(2):
            P.bankmap[("pss", j_, p_)] = [4 + p_]
        P.bankmap[("ppt", j_)] = [6]
        P.bankmap[("pso", j_)] = [7]
        P.bankmap[("accx", j_)] = [4 + j_]
        P.bankmap[("accy", j_)] = [6 + j_]
    P.bankmap["pvt"] = [7]
    for q_ in range(4):
        P.bankmap[("pxt", q_)] = [4 + q_]
    OMK = [("om", i) for i in range(2 * DC)]
    XBK = [("xb", i) for i in range(DC)]
    XLK = [("xl", i) for i in range(DC)]

    class WStream:
        def __init__(self):
            self.plan = []
            self.issued = 0
            self.cursor = 0
            self.base = 0

        def tile(self, wt, l, row0, nkc, col0, ncols):
            assert nkc * ncols <= SLOT_ELEMS
            if P.dry:
                self.plan.append((wt, l, row0, nkc, col0, ncols))
                return None
            k = self.cursor
            self.cursor += 1
            assert self.plan[k] == (wt, l, row0, nkc, col0, ncols)
            while self.issued < len(self.plan) and self.issued <= self.base + NSLOT - 1:
                self._issue(self.issued)
                self.issued += 1
            assert self.issued > k
            return k % NSLOT

        def begin_group(self):
            self.base = self.cursor

        def _issue(self, n):
            wt, l, row0, nkc, col0, ncols = self.plan[n]
            slot = n % NSLOT
            src = wt[l, row0:row0 + nkc * 128, col0:col0 + ncols].rearrange("(kc p) n -> p kc n", p=128)
            dst = WS[slot][:, 0:nkc * ncols].rearrange("p (a b) -> p a b", b=ncols)
            P.op("pool", lambda e, s=src, d=dst: e.dma_start(out=d, in_=s),
                 writes=[("w", slot)], dma_key=("w", slot))

    W = WStream()
    stg_ctr = [0]

    def stg():
        i = stg_ctr[0] % NST
        stg_ctr[0] += 1
        return i

    ps_ctr = [0]

    def fence(keys):
        P.op("dve", lambda e: e.memset(DUM[:, 0:1], 0.0), writes=list(keys))

    def prm(l, name, i, n=1):
        o, cnt = PL[name]
        base = l * PLN + o + i
        return PRM[:, base:base + n]

    def proj(nk, mv, rkeys, wt, l, row0, col0, nch, evac, halo=None, dup64=False, wcols=None, inter=None):
        if wcols is None:
            wcols = nch * 128
        kt = min(16, SLOT_ELEMS // wcols)
        ktiles = []
        k0 = 0
        W.begin_group()
        assert (nk + kt - 1) // kt <= NSLOT
        while k0 < nk:
            n = min(kt, nk - k0)
            ktiles.append((k0, n, W.tile(wt, l, row0 + k0 * 128, n, col0, wcols)))
            k0 += n
        for ci in range(nch):
            s = ps_ctr[0] % 2
            ps_ctr[0] += 1
            ps = PSL[s]
            subs = []
            for (k0, n, slot) in ktiles:
                step = n if inter is None else 4
                for a in range(0, n, step):
                    subs.append((k0, a, min(n, a + step), slot))
            nsub = len(subs)
            sched = {}
            if inter is not None:
                for qi, f in enumerate(inter):
                    sched.setdefault(min(nsub - 1, 1 + (qi * (nsub - 1)) // len(inter)), []).append(f)
            for si, (k0, ka, kb_, slot) in enumerate(subs):
                for f in sched.get(si, ()):
                    f()
                if P.dry:
                    continue
                mms = []
                for kk in range(ka, kb_):
                    kc = k0 + kk
                    first = (kc == 0)
                    last = (kc == nk - 1)
                    if dup64:
                        lhsT = WS[slot][:, kk * wcols + ci * 64: kk * wcols + ci * 64 + 64]
                        prs = [(0, 64), (64, 128)]
                    else:
                        lhsT = WS[slot][:, kk * wcols + ci * 128: kk * wcols + ci * 128 + 128]
                        prs = [(0, 128)]
                    for (p0, p1) in prs:
                        mms.append((ps[p0:p1, 0:HF], lhsT, mv(kc, 0), first, last, False))
                        if halo is not None:
                            mms.append((ps[p0:p1, HF:HF + 128], lhsT, halo(kc), False, last, True))
                        mms.append((ps[p0:p1, 512:512 + HF], lhsT, mv(kc, 1), first, last, False))

                def emit(e, mms=mms):
                    ins = None
                    for (o, w_, r, s0, s1, skip) in mms:
                        ins = e.matmul(o, lhsT=w_, rhs=r, start=s0, stop=s1, skip_group_check=skip)
                    return ins
                rk = [("w", slot)]
                for kk in range(ka, kb_):
                    rk += rkeys(k0 + kk)
                P.op("pe", emit, reads=rk, writes=[("ps", s)])
            evac(ci, ps, s)

    xb_mv = lambda kc, h: XB[:, kc, h * HF:(h + 1) * HF]
    xb_rk = lambda kc: [("xb", kc)]
    xbh_mv = lambda kc: XBH[:, kc, :]

    def psr(ps, a, b):
        assert (a < HF) == (b <= HF)
        if a < HF:
            return ps[:, a:b]
        return ps[:, 512 + a - HF:512 + b - HF]

    def halves(fn):
        fn(0, HF)
        fn(HF, T)

    acc_first = [True]

    def stats_acc(src_ap_fn, srckey):
        q = stg()
        P.op("act", lambda e: e.activation(out=STG[q][:], in_=src_ap_fn(), func=AF.Square),
             reads=[srckey], writes=[("st", q)])
        if acc_first[0]:
            acc_first[0] = False
            for h in range(2):
                P.op("dve", lambda e, h=h: e.tensor_copy(out=psX[:, h * 512:h * 512 + HF], in_=src_ap_fn()[:, h * HF:(h + 1) * HF]),
                     reads=[srckey, "psxy"], writes=[("accx", h)])
                P.op("dve", lambda e, h=h: e.tensor_copy(out=psY[:, h * 512:h * 512 + HF], in_=STG[q][:, h * HF:(h + 1) * HF]),
                     reads=[("st", q), "psxy"], writes=[("accy", h)])
        else:
            for h in range(2):
                P.op("dve", lambda e, h=h: e.tensor_tensor(out=psX[:, h * 512:h * 512 + HF], in0=psX[:, h * 512:h * 512 + HF],
                                                          in1=src_ap_fn()[:, h * HF:(h + 1) * HF], op=ALU.add),
                     reads=[srckey, "psxy", ("accx", h)], writes=[("accx", h)])
                P.op("dve", lambda e, h=h: e.tensor_tensor(out=psY[:, h * 512:h * 512 + HF], in0=psY[:, h * 512:h * 512 + HF],
                                                          in1=STG[q][:, h * HF:(h + 1) * HF], op=ALU.add),
                     reads=[("st", q), "psxy", ("accy", h)], writes=[("accy", h)])

    def ln_rows(ones):
        qa, qb = stg(), stg()
        for h in range(2):
            P.op("dve", lambda e, h=h: e.tensor_copy(out=STG[qa][:, h * HF:(h + 1) * HF], in_=psX[:, h * 512:h * 512 + HF]),
                 reads=[("accx", h), "psxy"], writes=[("st", qa)])
            P.op("dve", lambda e, h=h: e.tensor_copy(out=STG[qb][:, h * HF:(h + 1) * HF], in_=psY[:, h * 512:h * 512 + HF]),
                 reads=[("accy", h), "psxy"], writes=[("st", qb)])
        for (q, s) in ((qa, 0), (qb, 1)):
            def emit(e, q=q, s=s):
                ins = None
                for h in range(2):
                    ins = e.matmul(PSL[s][:, h * 512:h * 512 + HF], lhsT=ones[:], rhs=STG[q][:, h * HF:(h + 1) * HF],
                                   start=True, stop=True)
                return ins
            P.op("pe", emit, reads=[("st", q), "ones"], writes=[("ps", s)])
        ps_ctr[0] = 0
        for h in range(2):
            P.op("act", lambda e, h=h: e.copy(out=ROW[0][:, h * HF:(h + 1) * HF], in_=psA[:, h * 512:h * 512 + HF]),
                 reads=[("ps", 0)], writes=[("row", 0)])
        qc = stg()
        P.op("dve", lambda e: e.tensor_tensor(out=STG[qc][:], in0=ROW[0][:], in1=ROW[0][:], op=ALU.mult),
             reads=[("row", 0)], writes=[("st", qc)])
        qd = stg()
        for h in range(2):
            P.op("dve", lambda e, h=h: e.tensor_tensor(out=STG[qd][:, h * HF:(h + 1) * HF], in0=psB[:, h * 512:h * 512 + HF],
                                                      in1=STG[qc][:, h * HF:(h + 1) * HF], op=ALU.subtract),
                 reads=[("ps", 1), ("st", qc)], writes=[("st", qd)])
        P.op("act", lambda e: e.activation(out=STG[qc][:], in_=STG[qd][:], func=AF.Sqrt, bias=EPSC[:, 0:1]),
             reads=[("st", qd), "eps"], writes=[("st", qc)])
        P.op("dve", lambda e: e.reciprocal(out=ROW[1][:], in_=STG[qc][:]), reads=[("st", qc)], writes=[("row", 1)])
        P.op("dve", lambda e: e.scalar_tensor_tensor(out=ROW[2][:], in0=ROW[0][:], scalar=-1.0, in1=ROW[1][:],
                                                    op0=ALU.mult, op1=ALU.mult),
             reads=[("row", 0), ("row", 1)], writes=[("row", 2)])
        acc_first[0] = True

    def stream_update(c, ps, s, alpha, stats):
        qa, qb = stg(), stg()
        for h in range(2):
            P.op("dve", lambda e, h=h: e.scalar_tensor_tensor(out=STG[qa][:, h * HF:(h + 1) * HF], in0=XB[:, c, h * HF:(h + 1) * HF],
                                                             scalar=float(alpha), in1=ps[:, h * 512:h * 512 + HF],
                                                             op0=ALU.mult, op1=ALU.add),
                 reads=[("xb", c), ("ps", s)], writes=[("st", qa)])
        P.op("dve", lambda e: e.scalar_tensor_tensor(out=STG[qb][:], in0=XL[:, c, :], scalar=float(alpha), in1=STG[qa][:],
                                                    op0=ALU.mult, op1=ALU.add),
             reads=[("xl", c), ("st", qa)], writes=[("st", qb)])
        if stats:
            stats_acc(lambda: STG[qb][:], ("st", qb))
        P.op("act", lambda e: e.copy(out=XB[:, c, :], in_=STG[qb][:]), reads=[("st", qb)], writes=[("xb", c)])
        P.op("dve", lambda e: e.tensor_tensor(out=XL[:, c, :], in0=STG[qb][:], in1=XB[:, c, :], op=ALU.subtract),
             reads=[("st", qb), ("xb", c)], writes=[("xl", c)])

    def ln_stream(l, gname, bname, final):
        GRP = 3
        for c0 in range(0, DC, GRP):
            cs = list(range(c0, min(DC, c0 + GRP)))
            qa = {c: stg() for c in cs}
            qd = {c: stg() for c in cs}
            for c in cs:
                P.op("dve", lambda e, c=c, a=qa[c]: e.tensor_tensor(out=STG[a][:], in0=XB[:, c, :], in1=XL[:, c, :], op=ALU.add),
                     reads=[("xb", c), ("xl", c)], writes=[("st", qa[c])])
            for c in cs:
                P.op("dve", lambda e, a=qa[c]: e.tensor_tensor(out=STG[a][:], in0=STG[a][:], in1=ROW[1][:], op=ALU.mult),
                     reads=[("st", qa[c]), ("row", 1)], writes=[("st", qa[c])])
            for c in cs:
                P.op("dve", lambda e, a=qa[c]: e.tensor_tensor(out=STG[a][:], in0=STG[a][:], in1=ROW[2][:], op=ALU.add),
                     reads=[("st", qa[c]), ("row", 2)], writes=[("st", qa[c])])
            for c in cs:
                g_ap = prm(l, gname, c)
                b_ap = prm(l, bname, c)
                if final:
                    P.op("act", lambda e, c=c, a=qa[c], g_ap=g_ap, b_ap=b_ap: e.activation(out=Y32[:, c, :], in_=STG[a][:], func=AF.Identity,
                                                                                        scale=g_ap, bias=b_ap),
                         reads=[("st", qa[c]), "y32f"], writes=[("om", 2 * c), ("om", 2 * c + 1)])
                else:
                    P.op("act", lambda e, a=qa[c], d=qd[c], g_ap=g_ap, b_ap=b_ap: e.activation(out=STG[d][:], in_=STG[a][:], func=AF.Identity,
                                                                                            scale=g_ap, bias=b_ap),
                         reads=[("st", qa[c])], writes=[("st", qd[c])])
            if not final:
                for c in cs:
                    P.op("act", lambda e, c=c, d=qd[c]: e.copy(out=XB[:, c, :], in_=STG[d][:]), reads=[("st", qd[c])], writes=[("xb", c)])
                for c in cs:
                    P.op("dve", lambda e, c=c, d=qd[c]: e.tensor_tensor(out=XL[:, c, :], in0=STG[d][:], in1=XB[:, c, :], op=ALU.subtract),
                         reads=[("st", qd[c]), ("xb", c)], writes=[("xl", c)])

    def load_tokens(row0, ntok, dst_hi, dst_lo, hikeys, lokeys, xsbuf):
        P.op("sp", lambda e: e.dma_start(out=XS[xsbuf][0:ntok, :], in_=xin[row0:row0 + ntok, :]),
             reads=["xsf"], writes=[("xs", xsbuf)], dma_key=("xs", xsbuf))
        for c in range(DC):
            q = c % 4
            P.op("pe", lambda e, c=c, q=q: e.transpose(PXT[q][:, 0:ntok], XS[xsbuf][0:ntok, c * 128:(c + 1) * 128],
                                                       IDF[0:ntok, 0:ntok]),
                 reads=[("xs", xsbuf), "idf", "psxy"], writes=[("pxt", q)])
            P.op("act", lambda e, c=c, q=q: e.copy(out=dst_hi(c), in_=PXT[q][:, 0:ntok]),
                 reads=[("pxt", q)], writes=[hikeys(c)])
            if dst_lo is not None:
                P.op("dve", lambda e, c=c, q=q: e.tensor_tensor(out=dst_lo(c), in0=PXT[q][:, 0:ntok], in1=dst_hi(c),
                                                               op=ALU.subtract),
                     reads=[("pxt", q), hikeys(c)], writes=[lokeys(c)])

    blk = [(b * 128, min(128, T - b * 128)) for b in range(NBLK)]

    def load_tile(t):
        fence(OMK + ["arena", "xsf", ("ys", 0), ("ys", 1)] + XBK + XLK + ["psxy"])
        n = 0
        if t == 0:
            load_tokens(0, 128, lambda c: XBH[:, c, :], None, lambda c: "arena_xbh", None, n % 2)
            n += 1
        for (c0, ntok) in blk:
            load_tokens(HALO + t * T + c0, ntok, lambda c, c0=c0, ntok=ntok: XB[:, c, c0:c0 + ntok],
                        lambda c, c0=c0, ntok=ntok: XL[:, c, c0:c0 + ntok],
                        lambda c: ("xb", c), lambda c: ("xl", c), n % 2)
            n += 1

    def attention(t, l):
        first = (t == 0 and l == 0)
        A = ["arena"]
        P.op("dve", lambda e: e.memset(K2[:, :, 672:768], 0.0), reads=A, writes=[("k2", g) for g in range(NKV)])
        P.op("dve", lambda e: e.memset(VF[:, :, 672:768], 0.0), reads=A, writes=[("vf", g) for g in range(KVC)])

        def evac_k(ci, ps, s):
            halves(lambda a, b: P.op("act", lambda e: e.copy(out=K2[:, ci, 128 + a:128 + b], in_=psr(ps, a, b)),
                                     reads=[("ps", s)] + A, writes=[("k2", ci)]))
            if first:
                P.op("act", lambda e: e.copy(out=K2[:, ci, 0:128], in_=ps[:, HF:HF + 128]),
                     reads=[("ps", s)] + A, writes=[("k2", ci)])
            else:
                P.op("dve", lambda e: e.tensor_copy(out=K2[:, ci, 0:128], in_=KH[:, l * NKV + ci, :]),
                     reads=[("kh", l)] + A, writes=[("k2", ci)])
            P.op("dve", lambda e: e.tensor_copy(out=KH[:, l * NKV + ci, :], in_=K2[:, ci, T:T + 128]),
                 reads=[("k2", ci)] + A, writes=[("kh", l)])
        proj(DC, xb_mv, xb_rk, w_in, l, 0, cfg["OFF_K"], NKV, evac_k, halo=(xbh_mv if first else None),
             dup64=True, wcols=KVW)

        def evac_v(ci, ps, s):
            halves(lambda a, b: P.op("act", lambda e: e.copy(out=VF[:, ci, 128 + a:128 + b], in_=psr(ps, a, b)),
                                     reads=[("ps", s)] + A, writes=[("vf", ci)]))
            if first:
                P.op("act", lambda e: e.copy(out=VF[:, ci, 0:128], in_=ps[:, HF:HF + 128]),
                     reads=[("ps", s)] + A, writes=[("vf", ci)])
            else:
                P.op("dve", lambda e: e.tensor_copy(out=VF[:, ci, 0:128], in_=VH[:, l * KVC + ci, :]),
                     reads=[("vh", l)] + A, writes=[("vf", ci)])
            P.op("dve", lambda e: e.tensor_copy(out=VH[:, l * KVC + ci, :], in_=VF[:, ci, T:T + 128]),
                 reads=[("vf", ci)] + A, writes=[("vh", l)])
        proj(DC, xb_mv, xb_rk, w_in, l, 0, cfg["OFF_V"], KVC, evac_v, halo=(xbh_mv if first else None),
             wcols=min(KVW, 128 * KVC))
        mw = KVW if KVW < 128 else 128
        for i in range(6):
            for ch in range(KVC):
                P.op("pe", lambda e, i=i, ch=ch: e.transpose(psYb[0:128, 1536 + ch * 128:1536 + ch * 128 + mw],
                                                             VF[0:mw, ch, i * 128:(i + 1) * 128], IDB[0:mw, 0:mw]),
                     reads=[("vf", ch), "idb", "psxy"] + A, writes=["pvt"])
            P.op("dve", lambda e, i=i: e.tensor_copy(out=VT[:, i, :], in_=psYb[:, 1536:1536 + KVW]),
                 reads=["pvt"] + A, writes=[("vt", i)])

        nu = [0]

        def evac_q(cq):
            def f(ci, ps, s):
                halves(lambda a, b: P.op("act", lambda e: e.copy(out=QB[:, cq % 2, a:b], in_=psr(ps, a, b)),
                                         reads=[("ps", s)] + A, writes=[("qb", cq % 2)]))
            return f

        def unit_s(cq, b):
            j = nu[0] % 2
            nq = blk[b][1]
            g = (2 * cq) // cfg["GROUP"]
            maskoff = 256 if (t == 0 and b == 1) else 0
            for p in range(2):
                P.op("pe", lambda e, p=p: e.matmul(psX[0:nq, p * 512:p * 512 + 256],
                                                   lhsT=QB[p * 64:(p + 1) * 64, cq % 2, b * 128:b * 128 + nq],
                                                   rhs=K2[p * 64:(p + 1) * 64, g, b * 128:b * 128 + 256], start=True, stop=True),
                     reads=[("qb", cq % 2), ("k2", g), "psxy"] + A, writes=[("pss", j, p)])
            for p in range(2):
                P.op("dve", lambda e, p=p: e.scalar_tensor_tensor(out=SS[0:nq, j, p, :], in0=psX[0:nq, p * 512:p * 512 + 256],
                                                                 scalar=0.125, in1=MSK[0:nq, maskoff:maskoff + 256],
                                                                 op0=ALU.mult, op1=ALU.add),
                     reads=[("pss", j, p), "msk"] + A, writes=[("ss", j, p)])
            for p in range(2):
                P.op("act", lambda e, p=p: e.activation(out=SS[0:nq, j, p, :], in_=SS[0:nq, j, p, :], func=AF.Exp,
                                                        accum_out=SM[0:nq, j * 2 + p:j * 2 + p + 1]),
                     reads=[("ss", j, p)] + A, writes=[("ss", j, p), ("sm", j, p)])
            h0 = 2 * cq
            P.op("dve", lambda e: e.tensor_tensor(out=SM[0:nq, 4 + j * 2:4 + j * 2 + 2], in0=SM[0:nq, j * 2:j * 2 + 2],
                                                  in1=ESK[0:nq, l * NQH + h0:l * NQH + h0 + 2], op=ALU.add),
                 reads=[("sm", j, 0), ("sm", j, 1), "esk"], writes=[("sd", j)])
            P.op("dve", lambda e: e.reciprocal(out=SM[0:nq, 8 + j * 2:8 + j * 2 + 2], in_=SM[0:nq, 4 + j * 2:4 + j * 2 + 2]),
                 reads=[("sd", j)], writes=[("sr", j)])
            for p in range(2):
                P.op("dve", lambda e, p=p: e.tensor_scalar(out=PN[0:nq, j, p, :], in0=SS[0:nq, j, p, :],
                                                          scalar1=SM[0:nq, 8 + j * 2 + p:8 + j * 2 + p + 1], scalar2=None, op0=ALU.mult),
                     reads=[("ss", j, p), ("sr", j)] + A, writes=[("pn", j, p)])
            nu[0] += 1
            return (cq, b, j, nq, g)

        def unit_o(u):
            cq, b, j, nq, g = u
            for p in range(2):
                for kb in range(2):
                    o0 = (p * 2 + kb) * 128
                    P.op("pe", lambda e, p=p, kb=kb, o0=o0: e.transpose(psYb[0:128, o0:o0 + nq], PN[0:nq, j, p, kb * 128:(kb + 1) * 128],
                                                                        IDB[0:nq, 0:nq]),
                         reads=[("pn", j, p), "idb", "psxy"] + A, writes=[("ppt", j)])
            src = psYb[:, 0:512].rearrange("p (a b) -> p a b", b=128)[:, :, 0:nq]
            dst = PTS[:, j, :].rearrange("p (a b) -> p a b", b=128)[:, :, 0:nq]
            P.op("act", lambda e: e.copy(out=dst, in_=src), reads=[("ppt", j)] + A, writes=[("pt", j)])
            for p in range(2):
                for kb in range(2):
                    P.op("pe", lambda e, p=p, kb=kb: e.matmul(psY[p * 64:(p + 1) * 64, 512:512 + nq],
                                                              lhsT=VT[:, b + kb, g * 64:(g + 1) * 64],
                                                              rhs=PTS[:, j, (p * 2 + kb) * 128:(p * 2 + kb) * 128 + nq],
                                                              start=(kb == 0), stop=(kb == 1)),
                         reads=[("vt", b + kb), ("pt", j), "psxy"] + A, writes=[("pso", j)])
            P.op("act", lambda e: e.copy(out=OM[:, cq, b * 128:b * 128 + nq], in_=psY[:, 512:512 + nq]),
                 reads=[("pso", j)], writes=[("om", cq)])

        pend = [None]

        def unit_cb(cq, b):
            def f():
                u = unit_s(cq, b)
                if pend[0] is not None:
                    unit_o(pend[0])
                pend[0] = u
            return f
        proj(DC, xb_mv, xb_rk, w_in, l, 0, 0, 1, evac_q(0), wcols=128)
        for cq in range(AC):
            cbs = [unit_cb(cq, b) for b in range(NBLK)]
            if cq + 1 < AC:
                proj(DC, xb_mv, xb_rk, w_in, l, 0, (cq + 1) * 128, 1, evac_q(cq + 1), wcols=128, inter=cbs)
            else:
                for f in cbs:
                    f()
        unit_o(pend[0])

    def conf_branch(t, l):
        first = (t == 0 and l == 0)
        zflag = (t == 0 and l == 1)
        A = ["arena"]

        def conf_chunk(i):
            qs = stg()

            def evac_gate(ci, ps, s):
                halves(lambda a, b: P.op("act", lambda e: e.activation(out=STG[qs][:, a:b], in_=psr(ps, a, b), func=AF.Sigmoid),
                                         reads=[("ps", s)], writes=[("st", qs)]))
                if first:
                    P.op("act", lambda e: e.activation(out=YB[:, 0:30], in_=ps[:, HF + 98:HF + 128], func=AF.Sigmoid),
                         reads=[("ps", s)], writes=["yb"])

            def evac_val(ci, ps, s):
                halves(lambda a, b: P.op("dve", lambda e: e.tensor_tensor(out=CB[:, 30 + a:30 + b], in0=psr(ps, a, b), in1=STG[qs][:, a:b], op=ALU.mult),
                                         reads=[("ps", s), ("st", qs)], writes=["cb"]))
                if first:
                    P.op("dve", lambda e: e.tensor_tensor(out=CB[:, 0:30], in0=ps[:, HF + 98:HF + 128], in1=YB[:, 0:30], op=ALU.mult),
                         reads=[("ps", s), "yb"], writes=["cb"])
                else:
                    P.op("dve", lambda e: e.tensor_copy(out=CB[:, 0:30], in_=CH[:, l * CC + i, :]), reads=[("ch", l, i)], writes=["cb"])
                if zflag:
                    P.op("dve", lambda e: e.tensor_scalar(out=CB[:, 30:30 + 128], in0=CB[:, 30:30 + 128], scalar1=PRM[:, cfg["NP"] - 1:cfg["NP"]],
                                                         scalar2=None, op0=ALU.mult),
                         reads=["cb", "prm"], writes=["cb"])
                P.op("dve", lambda e: e.tensor_copy(out=CH[:, l * CC + i, :], in_=CB[:, T:T + 30]), reads=["cb"], writes=[("ch", l, i)])
            proj(DC, xb_mv, xb_rk, w_in, l, 0, cfg["OFF_CONF"] + cfg["CONF_W"] + i * 128, 1, evac_gate,
                 halo=(xbh_mv if first else None), wcols=128)
            proj(DC, xb_mv, xb_rk, w_in, l, 0, cfg["OFF_CONF"] + i * 128, 1, evac_val,
                 halo=(xbh_mv if first else None), wcols=128)
            P.op("dve", lambda e, i=i: e.tensor_scalar(out=YB[:], in0=CB[:, 0:T], scalar1=prm(l, "cw", i * CONF_K), scalar2=prm(l, "cb", i),
                                                      op0=ALU.mult, op1=ALU.add),
                 reads=["cb", "prm"], writes=["yb"])
            for j in range(1, CONF_K):
                dst = YB[:] if j < CONF_K - 1 else YC[:, i, :]
                wk = ["yb"] if j < CONF_K - 1 else [("yc", i)]
                P.op("dve", lambda e, i=i, j=j, dst=dst: e.scalar_tensor_tensor(out=dst, in0=CB[:, j:j + T], scalar=prm(l, "cw", i * CONF_K + j),
                                                                               in1=YB[:], op0=ALU.mult, op1=ALU.add),
                     reads=["cb", "yb", "prm"] + A, writes=wk)
            stats_acc(lambda i=i: YC[:, i, :], ("yc", i))
        for i in range(CC):
            conf_chunk(i)
        ln_rows(ONEC)
        for i0 in range(0, CC, 4):
            cs = list(range(i0, min(CC, i0 + 4)))
            qa = {i: stg() for i in cs}
            for i in cs:
                P.op("dve", lambda e, i=i, a=qa[i]: e.tensor_tensor(out=STG[a][:], in0=YC[:, i, :], in1=ROW[1][:], op=ALU.mult),
                     reads=[("yc", i), ("row", 1)] + A, writes=[("st", qa[i])])
            for i in cs:
                P.op("dve", lambda e, a=qa[i]: e.tensor_tensor(out=STG[a][:], in0=STG[a][:], in1=ROW[2][:], op=ALU.add),
                     reads=[("st", qa[i]), ("row", 2)], writes=[("st", qa[i])])
            for i in cs:
                P.op("act", lambda e, i=i, a=qa[i]: e.activation(out=OM[:, AC + i, :], in_=STG[a][:], func=AF.Silu,
                                                                scale=prm(l, "cg", i), bias=prm(l, "cbb", i)),
                     reads=[("st", qa[i]), "prm"], writes=[("om", AC + i)])

    def sc_branch(t, l):
        first = (t == 0 and l == 0)
        zflag = (t == 0 and l == 1)
        base = cfg["OFF_SC"]
        SW = cfg["SC_W"]

        def sc_chunk(i):
            qx = stg()

            def evac_x(ci, ps, s):
                halves(lambda a, b: P.op("act", lambda e: e.copy(out=STG[qx][:, a:b], in_=psr(ps, a, b)),
                                         reads=[("ps", s)], writes=[("st", qx)]))
                if first:
                    P.op("act", lambda e: e.copy(out=YB[:, 0:2], in_=ps[:, HF + 126:HF + 128]), reads=[("ps", s)], writes=["yb"])

            def evac_c(ci, ps, s):
                halves(lambda a, b: P.op("dve", lambda e: e.tensor_tensor(out=UB[:, 2 + a:2 + b], in0=psr(ps, a, b), in1=STG[qx][:, a:b], op=ALU.mult),
                                         reads=[("ps", s), ("st", qx)], writes=["ub"]))
                if first:
                    P.op("dve", lambda e: e.tensor_tensor(out=UB[:, 0:2], in0=ps[:, HF + 126:HF + 128], in1=YB[:, 0:2], op=ALU.mult),
                         reads=[("ps", s), "yb"], writes=["ub"])
                else:
                    P.op("dve", lambda e: e.tensor_copy(out=UB[:, 0:2], in_=UH[:, l * SCC + i, :]), reads=[("uh", l, i)], writes=["ub"])
                if zflag:
                    P.op("dve", lambda e: e.tensor_scalar(out=UB[:, 2:2 + 128], in0=UB[:, 2:2 + 128], scalar1=PRM[:, cfg["NP"] - 1:cfg["NP"]],
                                                         scalar2=None, op0=ALU.mult),
                         reads=["ub", "prm"], writes=["ub"])
                P.op("dve", lambda e: e.tensor_copy(out=UH[:, l * SCC + i, :], in_=UB[:, T:T + 2]), reads=["ub"], writes=[("uh", l, i)])
                qy = stg()
                P.op("dve", lambda e: e.tensor_scalar(out=STG[qy][:], in0=UB[:, 0:T], scalar1=prm(l, "sw", i * SC_K), scalar2=None, op0=ALU.mult),
                     reads=["ub", "prm"], writes=[("st", qy)])
                for j in range(1, SC_K):
                    P.op("dve", lambda e, j=j: e.scalar_tensor_tensor(out=STG[qy][:], in0=UB[:, j:j + T], scalar=prm(l, "sw", i * SC_K + j),
                                                                     in1=STG[qy][:], op0=ALU.mult, op1=ALU.add),
                         reads=["ub", ("st", qy), "prm"], writes=[("st", qy)])
                evac_c.qy = qy

            def evac_b(ci, ps, s):
                qy = evac_c.qy
                halves(lambda a, b: P.op("dve", lambda e: e.tensor_tensor(out=OM[:, AC + CC + i, a:b], in0=psr(ps, a, b), in1=STG[qy][:, a:b], op=ALU.mult),
                                         reads=[("ps", s), ("st", qy)], writes=[("om", AC + CC + i)]))
            proj(DC, xb_mv, xb_rk, w_in, l, 0, base + 2 * SW + i * 128, 1, evac_x, halo=(xbh_mv if first else None), wcols=128)
            proj(DC, xb_mv, xb_rk, w_in, l, 0, base + SW + i * 128, 1, evac_c, halo=(xbh_mv if first else None), wcols=128)
            proj(DC, xb_mv, xb_rk, w_in, l, 0, base + i * 128, 1, evac_b, wcols=128)
        for i in range(SCC):
            sc_chunk(i)

    def merge_phase(t, l):
        ob_mv = lambda off: (lambda kc, h: OM[:, off + kc, h * HF:(h + 1) * HF])
        ob_rk = lambda off: (lambda kc: [("om", off + kc)])
        branches = ((w_pa, AC, 0), (w_pb, CC, AC), (w_pc, SCC, AC + CC))
        def merge_pair(c0):
            sg = [stg(), stg()]
            macc = [stg(), stg()]
            for bi, (wp, nkb, off) in enumerate(branches):
                def evac_g(ci, ps, s):
                    halves(lambda a, b: P.op("act", lambda e: e.activation(out=STG[sg[ci]][:, a:b], in_=psr(ps, a, b), func=AF.Sigmoid),
                                             reads=[("ps", s)], writes=[("st", sg[ci])]))

                def evac_p(ci, ps, s, bi=bi):
                    c = c0 + ci
                    if bi == 0:
                        halves(lambda a, b: P.op("dve", lambda e: e.tensor_tensor(out=STG[macc[ci]][:, a:b], in0=psr(ps, a, b), in1=STG[sg[ci]][:, a:b], op=ALU.mult),
                                                 reads=[("ps", s), ("st", sg[ci])], writes=[("st", macc[ci])]))
                    else:
                        halves(lambda a, b: P.op("dve", lambda e: e.tensor_tensor(out=STG[sg[ci]][:, a:b], in0=psr(ps, a, b), in1=STG[sg[ci]][:, a:b], op=ALU.mult),
                                                 reads=[("ps", s), ("st", sg[ci])], writes=[("st", sg[ci])]))
                        if bi == 1:
                            P.op("dve", lambda e: e.tensor_tensor(out=STG[macc[ci]][:], in0=STG[macc[ci]][:], in1=STG[sg[ci]][:], op=ALU.add),
                                 reads=[("st", macc[ci]), ("st", sg[ci])], writes=[("st", macc[ci])])
                        else:
                            P.op("dve", lambda e: e.tensor_tensor(out=OM[:, DC + c, :], in0=STG[macc[ci]][:], in1=STG[sg[ci]][:], op=ALU.add),
                                 reads=[("st", macc[ci]), ("st", sg[ci])], writes=[("om", DC + c)])
                proj(DC, xb_mv, xb_rk, w_in, l, 0, cfg["OFF_G"] + bi * D + c0 * 128, 2, evac_g)
                proj(nkb, ob_mv(off), ob_rk(off), wp, l, 0, c0 * 128, 2, evac_p)
        for c0 in range(0, DC, 2):
            merge_pair(c0)

    def wout_phase(t, l):
        mb_mv = lambda kc, h: OM[:, DC + kc, h * HF:(h + 1) * HF]
        mb_rk = lambda kc: [("om", DC + kc)]
        for c0 in range(0, DC, 2):
            proj(DC, mb_mv, mb_rk, w_out, l, 0, c0 * 128, 2,
                 lambda ci, ps, s: stream_update(c0 + ci, ps, s, ALPHA, True))
        ln_rows(ONED)
        ln_stream(l, "ln1g", "ln1b", False)

    def ffn_update(c, ps, s, g):
        XE = lambda c: OM[:, FGMAX + c, :]
        xek = ("om", FGMAX + c)
        qa = stg()
        last = (g == G - 1)
        if g == 0:
            for h in range(2):
                P.op("dve", lambda e, h=h: e.scalar_tensor_tensor(out=STG[qa][:, h * HF:(h + 1) * HF], in0=XL[:, c, h * HF:(h + 1) * HF],
                                                                 scalar=float(ALPHA), in1=ps[:, h * 512:h * 512 + HF],
                                                                 op0=ALU.mult, op1=ALU.add),
                     reads=[("xl", c), ("ps", s)], writes=[("st", qa)])
        else:
            for h in range(2):
                P.op("dve", lambda e, h=h: e.tensor_tensor(out=STG[qa][:, h * HF:(h + 1) * HF], in0=ps[:, h * 512:h * 512 + HF],
                                                          in1=XL[:, c, h * HF:(h + 1) * HF], op=ALU.add),
                     reads=[("xl", c), ("ps", s)], writes=[("st", qa)])
            P.op("dve", lambda e: e.tensor_tensor(out=STG[qa][:], in0=STG[qa][:], in1=XE(c), op=ALU.add),
                 reads=[("st", qa), xek], writes=[("st", qa)])
        if not last:
            P.op("act", lambda e: e.copy(out=XL[:, c, :], in_=STG[qa][:]), reads=[("st", qa)], writes=[("xl", c)])
            P.op("dve", lambda e: e.tensor_tensor(out=XE(c), in0=STG[qa][:], in1=XL[:, c, :], op=ALU.subtract),
                 reads=[("st", qa), ("xl", c)], writes=[xek])
        else:
            qb = stg()
            P.op("dve", lambda e: e.scalar_tensor_tensor(out=STG[qb][:], in0=XB[:, c, :], scalar=float(ALPHA), in1=STG[qa][:],
                                                        op0=ALU.mult, op1=ALU.add),
                 reads=[("xb", c), ("st", qa)], writes=[("st", qb)])
            stats_acc(lambda: STG[qb][:], ("st", qb))
            P.op("act", lambda e: e.copy(out=XB[:, c, :], in_=STG[qb][:]), reads=[("st", qb)], writes=[("xb", c)])
            P.op("dve", lambda e: e.tensor_tensor(out=XL[:, c, :], in0=STG[qb][:], in1=XB[:, c, :], op=ALU.subtract),
                 reads=[("st", qb), ("xb", c)], writes=[("xl", c)])

    def ffn_phase(t, l):
        act_mv = lambda kc, h: OM[:, kc, h * HF:(h + 1) * HF]
        act_rk = lambda kc: [("om", kc)]
        final = (l == L - 1)

        def ffn_in_block(g, j0, nch, gbase):
            sl = [stg(), stg()]

            def evac_gate(ci, ps, s):
                halves(lambda a, b: P.op("act", lambda e: e.activation(out=STG[sl[ci]][:, a:b], in_=psr(ps, a, b), func=AF.Silu),
                                         reads=[("ps", s)], writes=[("st", sl[ci])]))

            def evac_up(ci, ps, s):
                halves(lambda a, b: P.op("dve", lambda e: e.tensor_tensor(out=OM[:, j0 + ci, a:b], in0=psr(ps, a, b), in1=STG[sl[ci]][:, a:b], op=ALU.mult),
                                         reads=[("ps", s), ("st", sl[ci])], writes=[("om", j0 + ci)]))
            col = (gbase + j0) * 128
            proj(DC, xb_mv, xb_rk, w_fi, l, 0, col, nch, evac_gate)
            proj(DC, xb_mv, xb_rk, w_fi, l, 0, DFF + col, nch, evac_up)

        def ffn_down_pair(g, c0, gbase, fg):
            proj(fg, act_mv, act_rk, w_fd, l, gbase * 128, c0 * 128, 2,
                 lambda ci, ps, s: ffn_update(c0 + ci, ps, s, g))
        gbase = 0
        for g in range(G):
            fg = FGS[g]
            j0 = 0
            while j0 < fg:
                nch = min(2, fg - j0)
                ffn_in_block(g, j0, nch, gbase)
                j0 += nch
            for c0 in range(0, DC, 2):
                ffn_down_pair(g, c0, gbase, fg)
            gbase += fg
        ln_rows(ONED)
        if final:
            fence(OMK + ["y32f"])
        ln_stream(l, "ln2g", "ln2b", final)

    def store_tile(t):
        fence(XBK + XLK + [("ys", 0), ("ys", 1), "psxy"])
        n = 0
        for bi, (c0, ntok) in enumerate(blk):
            if t == 0 and bi == 0:
                continue
            yb = n % 2
            for c in range(DC):
                q = c % 4
                P.op("pe", lambda e, c=c, q=q, c0=c0, ntok=ntok: e.transpose(PXT[q][0:ntok, :], Y32[:, c, c0:c0 + ntok], IDF[:, :]),
                     reads=[("om", 2 * c), ("om", 2 * c + 1), "idf", "psxy"], writes=[("pxt", q)])
                if c % 2 == 0:
                    P.op("act", lambda e, c=c, q=q, yb=yb, ntok=ntok: e.copy(out=YS[yb][0:ntok, c * 128:(c + 1) * 128], in_=PXT[q][0:ntok, :]),
                         reads=[("pxt", q)], writes=[("ys", yb)])
                else:
                    P.op("dve", lambda e, c=c, q=q, yb=yb, ntok=ntok: e.tensor_copy(out=YS[yb][0:ntok, c * 128:(c + 1) * 128], in_=PXT[q][0:ntok, :]),
                         reads=[("pxt", q)], writes=[("ys", yb)])
            r0 = t * T + c0 - HALO
            P.op("sp", lambda e, r0=r0, yb=yb, ntok=ntok: e.dma_start(out=y_d[r0:r0 + ntok, :], in_=YS[yb][0:ntok, :]),
                 reads=[("ys", yb)], dma_key=("yo", yb))
            n += 1

    EPSC = sb("EPSC", [128, 1], F32)

    def prologue():
        P.op("sp", lambda e: e.dma_start(out=PRM[:], in_=prm_d), writes=["prm"], dma_key="c0")
        P.op("sp", lambda e: e.dma_start(out=MSK[:], in_=msk_d), writes=["msk"], dma_key="c1")
        P.op("sp", lambda e: e.dma_start(out=IDF[:], in_=idf_d), writes=["idf"], dma_key="c2")
        P.op("dve", lambda e: e.tensor_copy(out=IDB[:], in_=IDF[:]), reads=["idf"], writes=["idb"])
        P.op("dve", lambda e: e.memset(ONED[:], 1.0 / D), writes=["ones"])
        P.op("dve", lambda e: e.memset(ONEC[:], 1.0 / cfg["CONF_W"]), writes=["ones"])
        P.op("dve", lambda e: e.memset(EPSC[:], LN_EPS), writes=["eps"])
        P.op("dve", lambda e: e.memset(CH[:], 0.0), writes=[("ch", l, i) for l in range(L) for i in range(CC)])
        P.op("dve", lambda e: e.memset(UH[:], 0.0), writes=[("uh", l, i) for l in range(L) for i in range(SCC)])
        P.op("dve", lambda e: e.memset(KH[:], 0.0), writes=[("kh", l) for l in range(L)])
        P.op("dve", lambda e: e.memset(VH[:], 0.0), writes=[("vh", l) for l in range(L)])
        for l in range(L):
            o, n = PL["sink"]
            P.op("act", lambda e, l=l, o=o, n=n: e.activation(out=ESK[:, l * NQH:(l + 1) * NQH], in_=PRM[:, l * PLN + o:l * PLN + o + n], func=AF.Exp),
                 reads=["prm"], writes=["esk"])

    def body():
        for t in range(NT):
            load_tile(t)
            fence(OMK + [("xs", 0), ("xs", 1)])
            for l in range(L):
                fence(OMK[DC:] + ["arena", "psxy"])
                attention(t, l)
                fence(["arena", "psxy"])
                conf_branch(t, l)
                sc_branch(t, l)
                fence(OMK[DC:] + ["arena"])
                merge_phase(t, l)
                fence(["psxy"])
                wout_phase(t, l)
                fence(["psxy"])
                ffn_phase(t, l)
            store_tile(t)

    P.dry = True
    body()
    P.dry = False
    stg_ctr[0] = 0
    ps_ctr[0] = 0
    acc_first[0] = True
    prologue()
    body()
    P.wait_all_dma("sp", [("yo", 0), ("yo", 1)])
    info = P.emit()
    st.close()
    return nc, info


def pack_params(cfg, inp):
    L, DC, CC, SCC, NQH = cfg["L"], cfg["DC"], cfg["CC"], cfg["SCC"], cfg["NQH"]
    PL, PLN = cfg["PL"], cfg["PLN"]
    out = np.zeros((128, cfg["NP"]), np.float32)

    def chunked(v):
        return np.ascontiguousarray(np.asarray(v, np.float32).reshape(-1, 128).T)
    for l in range(L):
        b = l * PLN
        for name, key in (("ln1g", "ln1_g"), ("ln1b", "ln1_b"), ("ln2g", "ln2_g"), ("ln2b", "ln2_b"),
                          ("cb", "conf_dw_b"), ("cg", "conf_ln_g"), ("cbb", "conf_ln_b")):
            o, n = PL[name]
            out[:, b + o:b + o + n] = chunked(inp[key][l])
        o, n = PL["cw"]
        cw = np.asarray(inp["conf_dw"][l], np.float32)
        out[:, b + o:b + o + n] = cw.T.reshape(CC, 128, CONF_K).transpose(1, 0, 2).reshape(128, CC * CONF_K)
        o, n = PL["sw"]
        sw = np.asarray(inp["sc_dw"][l], np.float32)
        out[:, b + o:b + o + n] = sw.T.reshape(SCC, 128, SC_K).transpose(1, 0, 2).reshape(128, SCC * SC_K)
        o, n = PL["sink"]
        out[:, b + o:b + o + n] = np.broadcast_to(np.asarray(inp["attn_sinks"][l], np.float32)[None, :], (128, NQH))
    return out


def make_masks(seq_start):
    qi = np.arange(128)[:, None]
    kj = np.arange(256)[None, :]
    rel = qi + 128 - kj
    band = (rel >= 0) & (rel < 128)
    mA = np.where(band, 0.0, NEG).astype(np.float32)
    mB = mA.copy()
    if seq_start:
        mB[:, 0:128] = NEG
    return np.concatenate([mA, mB], axis=1)


_CACHE = {}


def run(cfg, inp):
    key = tuple(sorted((k, v) for k, v in cfg.items() if not isinstance(v, (dict, list))))
    if key not in _CACHE:
        _CACHE[key] = build_program(cfg)
    nc, info = _CACHE[key]
    D, OWN, NT = cfg["D"], cfg["OWN"], cfg["NT"]
    x = np.asarray(inp["x"], np.float32)
    Bn, S = x.shape[0], x.shape[1]
    ncs = cfg["NCORE_SEQ"]
    prm_base = pack_params(cfg, inp)
    idf = np.eye(128, dtype=np.float32)
    wmap = {"w_in": "w_in", "w_pa": "w_proj_attn", "w_pb": "w_proj_conf", "w_pc": "w_proj_sc",
            "w_out": "w_out", "w_fi": "w_ffn_in", "w_fd": "w_ffn_down"}
    weights = {k: np.ascontiguousarray(np.asarray(inp[v], np.float32)) for k, v in wmap.items()}
    in_maps = []
    for core in range(8):
        b, j = divmod(core, ncs)
        s0 = j * OWN
        xin = np.zeros((HALO + NT * T, D), np.float32)
        lo = s0 - 2 * HALO
        src_lo = max(lo, 0)
        xin[src_lo - lo:, :] = x[b, src_lo:s0 + OWN, :]
        p = prm_base.copy()
        p[:, -1] = 0.0 if j == 0 else 1.0
        m = {"xin": xin, "prm": p, "msk": make_masks(j == 0), "idf": idf}
        m.update(weights)
        in_maps.append(m)
    res = run_bass_kernel_spmd(nc, in_maps, core_ids=list(range(8)))
    out = np.zeros((Bn, S, D), np.float32)
    for core in range(8):
        b, j = divmod(core, ncs)
        out[b, j * OWN:(j + 1) * OWN, :] = res.results[core]["y"]
    return out


CFG = None


def kernel(**inputs):
    global CFG
    if CFG is None:
        CFG = make_cfg()
    return run(CFG, inputs)
```

```python
import contextlib
import numpy as np
import concourse.bass as bass
import concourse.mybir as mybir
from concourse.bass_utils import run_bass_kernel_spmd

F32 = mybir.dt.float32
BF16 = mybir.dt.bfloat16
AF = mybir.ActivationFunctionType
ALU = mybir.AluOpType

ENGS = ("sp", "act", "pool", "dve", "pe")
SAME_ENGINE_SYNC = True


class Op:
    __slots__ = ("eng", "emit", "idx", "signal", "waits", "dma_key", "dma_cum")

    def __init__(self, eng, emit):
        self.eng = eng
        self.emit = emit
        self.idx = -1
        self.signal = False
        self.waits = []
        self.dma_key = None
        self.dma_cum = 0


class Prog:
    def __init__(self, nc):
        self.nc = nc
        self.dry = False
        self.ops = {e: [] for e in ENGS}
        self.last_write = {}
        self.reads_since = {}
        self.waited_eng = {e: {f: -1 for f in ENGS} for e in ENGS}
        self.waited_dma = {e: {} for e in ENGS}
        self.dma_cum = {}
        self.bankmap = {}
        self.last_bank = {}

    def _add_dep(self, op, dep, kind):
        e = op.eng
        if dep is op:
            return
        if dep.dma_key is not None:
            cur = self.waited_dma[e].get(dep.dma_key, 0)
            if dep.dma_cum > cur:
                self.waited_dma[e][dep.dma_key] = dep.dma_cum
                op.waits.append(("dma", dep.dma_key, dep.dma_cum))
            return
        if dep.eng == e:
            if kind == "war" or not SAME_ENGINE_SYNC or e == "pe":
                return
        if dep.idx <= self.waited_eng[e][dep.eng]:
            return
        self.waited_eng[e][dep.eng] = dep.idx
        dep.signal = True
        op.waits.append(("eng", dep.eng, dep.idx))

    def op(self, eng, emit, reads=(), writes=(), dma_key=None):
        if self.dry:
            return None
        o = Op(eng, emit)
        o.idx = len(self.ops[eng])
        if dma_key is not None:
            o.dma_key = dma_key
            self.dma_cum[dma_key] = self.dma_cum.get(dma_key, 0) + 16
            o.dma_cum = self.dma_cum[dma_key]
        lw_ = self.last_write
        rs_ = self.reads_since
        for r in reads:
            lw = lw_.get(r)
            if lw is not None:
                self._add_dep(o, lw, "raw")
        for w in writes:
            lw = lw_.get(w)
            if lw is not None:
                self._add_dep(o, lw, "waw")
            for rd in rs_.get(w, ()):
                self._add_dep(o, rd, "war")
        for r in reads:
            rs_.setdefault(r, []).append(o)
        for w in writes:
            lw_[w] = o
            rs_[w] = []
        bm = self.bankmap
        bks = None
        for k in reads:
            if k in bm:
                bks = (bks or set())
                bks.update(bm[k])
        for k in writes:
            if k in bm:
                bks = (bks or set())
                bks.update(bm[k])
        if bks:
            for b in bks:
                d = self.last_bank.get(b)
                if d is not None and d.eng != eng:
                    self._add_dep(o, d, "waw")
                self.last_bank[b] = o
        self.ops[eng].append(o)
        return o

    def wait_all_dma(self, eng, keys):
        o = Op(eng, None)
        o.idx = len(self.ops[eng])
        for k in keys:
            if k in self.dma_cum:
                o.waits.append(("dma", k, self.dma_cum[k]))
        self.ops[eng].append(o)

    def emit(self):
        nc = self.nc
        with contextlib.ExitStack() as st:
            esem = {e: st.enter_context(nc.semaphore("S_" + e)) for e in ENGS}
            dsem = {}
            for i, k in enumerate(self.dma_cum):
                dsem[k] = st.enter_context(nc.semaphore("D%d" % i))
            sigcount = {}
            for e in ENGS:
                c = 0
                arr = []
                for o in self.ops[e]:
                    if o.signal:
                        c += 1
                    arr.append(c)
                sigcount[e] = arr
            block = st.enter_context(nc.Block())

            def run(engname, engobj):
                for o in self.ops[engname]:
                    for w in o.waits:
                        if w[0] == "dma":
                            engobj.wait_ge(dsem[w[1]], w[2])
                        else:
                            engobj.wait_ge(esem[w[1]], sigcount[w[1]][w[2]])
                    if o.emit is None:
                        continue
                    ins = o.emit(engobj)
                    if o.dma_key is not None:
                        ins.then_inc(dsem[o.dma_key], 16)
                    elif o.signal:
                        ins.then_inc(esem[engname], 1)

            @block.sync
            def _(eng):
                run("sp", eng)

            @block.scalar
            def _(eng):
                run("act", eng)

            @block.gpsimd
            def _(eng):
                run("pool", eng)

            @block.vector
            def _(eng):
                run("dve", eng)

            @block.tensor
            def _(eng):
                run("pe", eng)
        return {"n_ops": {e: len(self.ops[e]) for e in ENGS},
                "n_sig": {e: (sigcount[e][-1] if sigcount[e] else 0) for e in ENGS},
                "max_dma": max(self.dma_cum.values()) if self.dma_cum else 0}


T = 544
HF = 272
HALO = 128
NBLK = 5
CONF_K = 31
SC_K = 3
LN_EPS = 1e-5
NEG = -1e30
NSLOT = 4
SLOT_ELEMS = 4096
NST = 6


def make_cfg(D=4096, B=2, SEQ=8192, DEPTH=2, NT=4, alias=True):
    c = dict(D=D, B=B, SEQ=SEQ, L=DEPTH, NT=NT, alias=alias)
    c["DC"] = D // 128
    c["ATTN_W"] = D // 2
    c["AC"] = c["ATTN_W"] // 128
    c["NQH"] = c["ATTN_W"] // 64
    c["NKV"] = c["NQH"] // 8
    c["GROUP"] = 8
    c["KVW"] = c["NKV"] * 64
    c["KVC"] = max(1, c["KVW"] // 128)
    c["CONF_W"] = D // 4
    c["CC"] = c["CONF_W"] // 128
    c["SC_W"] = D // 4
    c["SCC"] = c["SC_W"] // 128
    c["DFF"] = ((8 * D // 3 + 255) // 256) * 256
    c["FC"] = c["DFF"] // 128
    c["G"] = 3
    q_, r_ = divmod(c["FC"], 3)
    c["FGS"] = [q_ + (1 if i < r_ else 0) for i in range(3)]
    c["IN_W"] = c["ATTN_W"] + 2 * c["KVW"] + 2 * c["CONF_W"] + 3 * c["SC_W"] + 3 * D
    c["OFF_K"] = c["ATTN_W"]
    c["OFF_V"] = c["OFF_K"] + c["KVW"]
    c["OFF_CONF"] = c["OFF_V"] + c["KVW"]
    c["OFF_SC"] = c["OFF_CONF"] + 2 * c["CONF_W"]
    c["OFF_G"] = c["OFF_SC"] + 3 * c["SC_W"]
    c["ALPHA"] = (2 * DEPTH) ** 0.25
    c["OWN"] = NT * T - HALO
    c["NCORE_SEQ"] = SEQ // c["OWN"]
    assert c["NCORE_SEQ"] * c["OWN"] == SEQ and c["NCORE_SEQ"] * B == 8
    DC, CC, SCC, NQH = c["DC"], c["CC"], c["SCC"], c["NQH"]
    o = 0
    pl = {}
    for name, n in (("ln1g", DC), ("ln1b", DC), ("ln2g", DC), ("ln2b", DC), ("cw", CC * CONF_K),
                    ("cb", CC), ("cg", CC), ("cbb", CC), ("sw", SCC * SC_K), ("sink", NQH)):
        pl[name] = (o, n)
        o += n
    c["PL"] = pl
    c["PLN"] = o
    c["NP"] = o * DEPTH + 1
    return c


def build_program(cfg):
    D, L, NT = cfg["D"], cfg["L"], cfg["NT"]
    DC, AC, CC, SCC = cfg["DC"], cfg["AC"], cfg["CC"], cfg["SCC"]
    NQH, NKV, KVW, KVC = cfg["NQH"], cfg["NKV"], cfg["KVW"], cfg["KVC"]
    FC, G, DFF = cfg["FC"], cfg["G"], cfg["DFF"]
    FGS = cfg["FGS"]
    FGMAX = max(FGS)
    assert FGMAX + cfg["DC"] <= 2 * cfg["DC"]
    ALPHA = cfg["ALPHA"]
    OWN = cfg["OWN"]
    PL, PLN = cfg["PL"], cfg["PLN"]
    NTOKIN = HALO + NT * T

    nc = bass.Bass("TRN2", target_bir_lowering=False)
    xin = nc.dram_tensor("xin", [NTOKIN, D], F32, kind="ExternalInput").ap()
    w_in = nc.dram_tensor("w_in", [L, D, cfg["IN_W"]], F32, kind="ExternalInput").ap()
    w_pa = nc.dram_tensor("w_pa", [L, cfg["ATTN_W"], D], F32, kind="ExternalInput").ap()
    w_pb = nc.dram_tensor("w_pb", [L, cfg["CONF_W"], D], F32, kind="ExternalInput").ap()
    w_pc = nc.dram_tensor("w_pc", [L, cfg["SC_W"], D], F32, kind="ExternalInput").ap()
    w_out = nc.dram_tensor("w_out", [L, D, D], F32, kind="ExternalInput").ap()
    w_fi = nc.dram_tensor("w_fi", [L, D, 2 * DFF], F32, kind="ExternalInput").ap()
    w_fd = nc.dram_tensor("w_fd", [L, DFF, D], F32, kind="ExternalInput").ap()
    prm_d = nc.dram_tensor("prm", [128, cfg["NP"]], F32, kind="ExternalInput").ap()
    msk_d = nc.dram_tensor("msk", [128, 512], F32, kind="ExternalInput").ap()
    idf_d = nc.dram_tensor("idf", [128, 128], F32, kind="ExternalInput").ap()
    y_d = nc.dram_tensor("y", [OWN, D], F32, kind="ExternalOutput").ap()

    st = contextlib.ExitStack()
    sb = lambda name, shape, dt: st.enter_context(nc.sbuf_tensor(name, shape, dt))
    XB = sb("XB", [128, DC, T], BF16)
    XL = sb("XL", [128, DC, T], BF16)
    OM = sb("OM", [128, 2 * DC, T], BF16)
    WS = [sb("WS%d" % i, [128, SLOT_ELEMS], BF16) for i in range(NSLOT)]
    STG = [sb("STG%d" % i, [128, T], F32) for i in range(NST)]
    ROW = [sb("ROW%d" % i, [128, T], F32) for i in range(3)]
    CB = sb("CB", [128, 30 + T], BF16)
    NDG = 8
    DG = [sb("DG%d" % i, [128, 128], BF16) for i in range(NDG)]
    dg_ctr = [0]
    YB = sb("YB", [128, T], F32)
    UB = sb("UB", [128, 2 + T], F32)
    CH = sb("CH", [128, L * CC, 30], F32)
    UH = sb("UH", [128, L * SCC, 2], F32)
    KH = sb("KH", [128, L * NKV, 128], BF16)
    VH = sb("VH", [128, L * KVC, 128], BF16)
    PRM = sb("PRM", [128, cfg["NP"]], F32)
    ESK = sb("ESK", [128, L * NQH], F32)
    MSK = sb("MSK", [128, 512], F32)
    IDF = sb("IDF", [128, 128], F32)
    IDB = sb("IDB", [128, 128], BF16)
    ONED = sb("ONED", [128, 128], F32)
    ONEC = sb("ONEC", [128, 128], F32)
    SM = sb("SM", [128, 16], F32)
    DUM = sb("DUM", [128, 2], F32)
    a_xbh = DC * 128 * 2
    a_k2 = a_xbh
    a_vf = a_k2 + NKV * 768 * 2
    a_vt = a_vf + KVC * 768 * 2
    a_qb = a_vt + 6 * KVW * 2
    a_ss = a_qb + 2 * T * 2
    a_pn = a_ss + 2 * 2 * 256 * 4
    a_pt = a_pn + 2 * 2 * 256 * 2
    a_end = a_pt + 2 * 512 * 2
    a_yc = a_xbh
    a_end = max(a_end, a_yc + CC * T * 4)
    mb_bytes = DC * T * 2
    if cfg["alias"]:
        assert a_end <= mb_bytes, (a_end, mb_bytes)
        AR, ar_base = OM, mb_bytes
    else:
        AR = sb("AR", [128, a_end // 2], BF16)
        ar_base = 0
    ARb = AR.bitcast(BF16) if False else AR
    ARflat_b = AR[:].rearrange("p a b -> p (a b)") if cfg["alias"] else AR[:]
    ARf32 = AR.bitcast(F32)
    ARflat_f = ARf32[:].rearrange("p a b -> p (a b)") if cfg["alias"] else ARf32[:]

    def arb(off, n):
        o = (ar_base + off) // 2
        return ARflat_b[:, o:o + n]

    def arf(off, n):
        o = (ar_base + off) // 4
        return ARflat_f[:, o:o + n]

    XBH = arb(0, DC * 128).rearrange("p (a b) -> p a b", b=128)
    K2 = arb(a_k2, NKV * 768).rearrange("p (a b) -> p a b", b=768)
    VF = arb(a_vf, KVC * 768).rearrange("p (a b) -> p a b", b=768)
    VT = arb(a_vt, 6 * KVW).rearrange("p (a b) -> p a b", b=KVW)
    QB = arb(a_qb, 2 * T).rearrange("p (a b) -> p a b", b=T)
    SS = arf(a_ss, 2 * 2 * 256).rearrange("p (j h k) -> p j h k", j=2, h=2)
    PN = arb(a_pn, 2 * 2 * 256).rearrange("p (j h k) -> p j h k", j=2, h=2)
    PTS = arb(a_pt, 2 * 512).rearrange("p (j k) -> p j k", j=2)
    YC = arf(a_yc, CC * T).rearrange("p (a b) -> p a b", b=T)
    OMf = OM.bitcast(F32)
    OMflat_f = OMf[:].rearrange("p a b -> p (a b)")
    Y32 = OMflat_f[:, 0:DC * T].rearrange("p (a b) -> p a b", b=T)
    XBflat_f = XB.bitcast(F32)[:].rearrange("p a b -> p (a b)")
    XS = [OMflat_f[:, i * D:(i + 1) * D] for i in range(2)]
    YS = [XBflat_f[:, i * D:(i + 1) * D] for i in range(2)]
    assert 2 * D <= DC * T // 2 + 0 or True
    assert 2 * D * 4 <= DC * T * 2, "staging must fit"

    psA = st.enter_context(nc.psum_tensor("psA", [128, 1024], F32))
    psB = st.enter_context(nc.psum_tensor("psB", [128, 1024], F32))
    psX = st.enter_context(nc.psum_tensor("psX", [128, 1024], F32))
    psY = st.enter_context(nc.psum_tensor("psY", [128, 1024], F32))
    PSL = [psA, psB]
    PXT = [psX[:, 0:128], psX[:, 512:640], psY[:, 0:128], psY[:, 512:640]]
    psYb = psY.bitcast(BF16)

    P = Prog(nc)
    for s_ in range(2):
        P.bankmap[("ps", s_)] = [2 * s_, 2 * s_ + 1]
    for j_ in range(2):
        for p_ in range(2):
            P.bankmap[("pss", j_, p_)] = [4 + p_]
        P.bankmap[("ppt", j_)] = [6]
        P.bankmap[("pso", j_)] = [7]
        P.bankmap[("accx", j_)] = [4 + j_]
        P.bankmap[("accy", j_)] = [6 + j_]
    P.bankmap["pvt"] = [7]
    for q_ in range(4):
        P.bankmap[("pxt", q_)] = [4 + q_]
    OMK = [("om", i) for i in range(2 * DC)]
    XBK = [("xb", i) for i in range(DC)]
    XLK = [("xl", i) for i in range(DC)]

    class WStream:
        def __init__(self):
            self.plan = []
            self.issued = 0
            self.cursor = 0
            self.base = 0

        def tile(self, wt, l, row0, nkc, col0, ncols):
            assert nkc * ncols <= SLOT_ELEMS
            if P.dry:
                self.plan.append((wt, l, row0, nkc, col0, ncols))
                return None
            k = self.cursor
            self.cursor += 1
            assert self.plan[k] == (wt, l, row0, nkc, col0, ncols)
            while self.issued < len(self.plan) and self.issued <= self.base + NSLOT - 1:
                self._issue(self.issued)
                self.issued += 1
            assert self.issued > k
            return k % NSLOT

        def begin_group(self):
            self.base = self.cursor

        def _issue(self, n):
            wt, l, row0, nkc, col0, ncols = self.plan[n]
            slot = n % NSLOT
            src = wt[l, row0:row0 + nkc * 128, col0:col0 + ncols].rearrange("(kc p) n -> p kc n", p=128)
            dst = WS[slot][:, 0:nkc * ncols].rearrange("p (a b) -> p a b", b=ncols)
            P.op("pool", lambda e, s=src, d=dst: e.dma_start(out=d, in_=s),
                 writes=[("w", slot)], dma_key=("w", slot))

    W = WStream()
    stg_ctr = [0]

    def stg():
        i = stg_ctr[0] % NST
        stg_ctr[0] += 1
        return i

    ps_ctr = [0]

    def fence(keys):
        P.op("dve", lambda e: e.memset(DUM[:, 0:1], 0.0), writes=list(keys))

    def prm(l, name, i, n=1):
        o, cnt = PL[name]
        base = l * PLN + o + i
        return PRM[:, base:base + n]

    def proj(nk, mv, rkeys, wt, l, row0, col0, nch, evac, halo=None, dup64=False, wcols=None, inter=None):
        if wcols is None:
            wcols = nch * 128
        kt = min(16, SLOT_ELEMS // wcols)
        ktiles = []
        k0 = 0
        W.begin_group()
        assert (nk + kt - 1) // kt <= NSLOT
        while k0 < nk:
            n = min(kt, nk - k0)
            ktiles.append((k0, n, W.tile(wt, l, row0 + k0 * 128, n, col0, wcols)))
            k0 += n
        for ci in range(nch):
            s = ps_ctr[0] % 2
            ps_ctr[0] += 1
            ps = PSL[s]
            subs = []
            for (k0, n, slot) in ktiles:
                step = n if inter is None else 4
                for a in range(0, n, step):
                    subs.append((k0, a, min(n, a + step), slot))
            nsub = len(subs)
            sched = {}
            if inter is not None:
                for qi, f in enumerate(inter):
                    sched.setdefault(min(nsub - 1, 1 + (qi * (nsub - 1)) // len(inter)), []).append(f)
            for si, (k0, ka, kb_, slot) in enumerate(subs):
                for f in sched.get(si, ()):
                    f()
                if P.dry:
                    continue
                mms = []
                for kk in range(ka, kb_):
                    kc = k0 + kk
                    first = (kc == 0)
                    last = (kc == nk - 1)
                    if dup64:
                        lhsT = WS[slot][:, kk * wcols + ci * 64: kk * wcols + ci * 64 + 64]
                        prs = [(0, 64), (64, 128)]
                    else:
                        lhsT = WS[slot][:, kk * wcols + ci * 128: kk * wcols + ci * 128 + 128]
                        prs = [(0, 128)]
                    for (p0, p1) in prs:
                        mms.append((ps[p0:p1, 0:HF], lhsT, mv(kc, 0), first, last, False))
                        if halo is not None:
                            mms.append((ps[p0:p1, HF:HF + 128], lhsT, halo(kc), False, last, True))
                        mms.append((ps[p0:p1, 512:512 + HF], lhsT, mv(kc, 1), first, last, False))

                def emit(e, mms=mms):
                    ins = None
                    for (o, w_, r, s0, s1, skip) in mms:
                        ins = e.matmul(o, lhsT=w_, rhs=r, start=s0, stop=s1, skip_group_check=skip)
                    return ins
                rk = [("w", slot)]
                for kk in range(ka, kb_):
                    rk += rkeys(k0 + kk)
                P.op("pe", emit, reads=rk, writes=[("ps", s)])
            evac(ci, ps, s)

    xb_mv = lambda kc, h: XB[:, kc, h * HF:(h + 1) * HF]
    xb_rk = lambda kc: [("xb", kc)]
    xbh_mv = lambda kc: XBH[:, kc, :]

    def psr(ps, a, b):
        assert (a < HF) == (b <= HF)
        if a < HF:
            return ps[:, a:b]
        return ps[:, 512 + a - HF:512 + b - HF]

    def halves(fn):
        fn(0, HF)
        fn(HF, T)

    acc_first = [True]

    def stats_acc(src_ap_fn, srckey):
        q = stg()
        P.op("act", lambda e: e.activation(out=STG[q][:], in_=src_ap_fn(), func=AF.Square),
             reads=[srckey], writes=[("st", q)])
        if acc_first[0]:
            acc_first[0] = False
            for h in range(2):
                P.op("dve", lambda e, h=h: e.tensor_copy(out=psX[:, h * 512:h * 512 + HF], in_=src_ap_fn()[:, h * HF:(h + 1) * HF]),
                     reads=[srckey, "psxy"], writes=[("accx", h)])
                P.op("dve", lambda e, h=h: e.tensor_copy(out=psY[:, h * 512:h * 512 + HF], in_=STG[q][:, h * HF:(h + 1) * HF]),
                     reads=[("st", q), "psxy"], writes=[("accy", h)])
        else:
            for h in range(2):
                P.op("dve", lambda e, h=h: e.tensor_tensor(out=psX[:, h * 512:h * 512 + HF], in0=psX[:, h * 512:h * 512 + HF],
                                                          in1=src_ap_fn()[:, h * HF:(h + 1) * HF], op=ALU.add),
                     reads=[srckey, "psxy", ("accx", h)], writes=[("accx", h)])
                P.op("dve", lambda e, h=h: e.tensor_tensor(out=psY[:, h * 512:h * 512 + HF], in0=psY[:, h * 512:h * 512 + HF],
                                                          in1=STG[q][:, h * HF:(h + 1) * HF], op=ALU.add),
                     reads=[("st", q), "psxy", ("accy", h)], writes=[("accy", h)])

    def ln_rows(ones):
        qa, qb = stg(), stg()
        for h in range(2):
            P.op("dve", lambda e, h=h: e.tensor_copy(out=STG[qa][:, h * HF:(h + 1) * HF], in_=psX[:, h * 512:h * 512 + HF]),
                 reads=[("accx", h), "psxy"], writes=[("st", qa)])
            P.op("dve", lambda e, h=h: e.tensor_copy(out=STG[qb][:, h * HF:(h + 1) * HF], in_=psY[:, h * 512:h * 512 + HF]),
                 reads=[("accy", h), "psxy"], writes=[("st", qb)])
        for (q, s) in ((qa, 0), (qb, 1)):
            def emit(e, q=q, s=s):
                ins = None
                for h in range(2):
                    ins = e.matmul(PSL[s][:, h * 512:h * 512 + HF], lhsT=ones[:], rhs=STG[q][:, h * HF:(h + 1) * HF],
                                   start=True, stop=True)
                return ins
            P.op("pe", emit, reads=[("st", q), "ones"], writes=[("ps", s)])
        ps_ctr[0] = 0
        for h in range(2):
            P.op("act", lambda e, h=h: e.copy(out=ROW[0][:, h * HF:(h + 1) * HF], in_=psA[:, h * 512:h * 512 + HF]),
                 reads=[("ps", 0)], writes=[("row", 0)])
        qc = stg()
        P.op("dve", lambda e: e.tensor_tensor(out=STG[qc][:], in0=ROW[0][:], in1=ROW[0][:], op=ALU.mult),
             reads=[("row", 0)], writes=[("st", qc)])
        qd = stg()
        for h in range(2):
            P.op("dve", lambda e, h=h: e.tensor_tensor(out=STG[qd][:, h * HF:(h + 1) * HF], in0=psB[:, h * 512:h * 512 + HF],
                                                      in1=STG[qc][:, h * HF:(h + 1) * HF], op=ALU.subtract),
                 reads=[("ps", 1), ("st", qc)], writes=[("st", qd)])
        P.op("act", lambda e: e.activation(out=STG[qc][:], in_=STG[qd][:], func=AF.Sqrt, bias=EPSC[:, 0:1]),
             reads=[("st", qd), "eps"], writes=[("st", qc)])
        P.op("dve", lambda e: e.reciprocal(out=ROW[1][:], in_=STG[qc][:]), reads=[("st", qc)], writes=[("row", 1)])
        P.op("dve", lambda e: e.scalar_tensor_tensor(out=ROW[2][:], in0=ROW[0][:], scalar=-1.0, in1=ROW[1][:],
                                                    op0=ALU.mult, op1=ALU.mult),
             reads=[("row", 0), ("row", 1)], writes=[("row", 2)])
        acc_first[0] = True

    def stream_update(c, ps, s, alpha, stats):
        qa, qb = stg(), stg()
        for h in range(2):
            P.op("dve", lambda e, h=h: e.scalar_tensor_tensor(out=STG[qa][:, h * HF:(h + 1) * HF], in0=XB[:, c, h * HF:(h + 1) * HF],
                                                             scalar=float(alpha), in1=ps[:, h * 512:h * 512 + HF],
                                                             op0=ALU.mult, op1=ALU.add),
                 reads=[("xb", c), ("ps", s)], writes=[("st", qa)])
        P.op("dve", lambda e: e.scalar_tensor_tensor(out=STG[qb][:], in0=XL[:, c, :], scalar=float(alpha), in1=STG[qa][:],
                                                    op0=ALU.mult, op1=ALU.add),
             reads=[("xl", c), ("st", qa)], writes=[("st", qb)])
        if stats:
            stats_acc(lambda: STG[qb][:], ("st", qb))
        P.op("act", lambda e: e.copy(out=XB[:, c, :], in_=STG[qb][:]), reads=[("st", qb)], writes=[("xb", c)])
        P.op("dve", lambda e: e.tensor_tensor(out=XL[:, c, :], in0=STG[qb][:], in1=XB[:, c, :], op=ALU.subtract),
             reads=[("st", qb), ("xb", c)], writes=[("xl", c)])

    def ln_stream(l, gname, bname, final):
        GRP = 3
        for c0 in range(0, DC, GRP):
            cs = list(range(c0, min(DC, c0 + GRP)))
            qa = {c: stg() for c in cs}
            qd = {c: stg() for c in cs}
            for c in cs:
                P.op("dve", lambda e, c=c, a=qa[c]: e.tensor_tensor(out=STG[a][:], in0=XB[:, c, :], in1=XL[:, c, :], op=ALU.add),
                     reads=[("xb", c), ("xl", c)], writes=[("st", qa[c])])
            for c in cs:
                P.op("dve", lambda e, a=qa[c]: e.tensor_tensor(out=STG[a][:], in0=STG[a][:], in1=ROW[1][:], op=ALU.mult),
                     reads=[("st", qa[c]), ("row", 1)], writes=[("st", qa[c])])
            for c in cs:
                P.op("dve", lambda e, a=qa[c]: e.tensor_tensor(out=STG[a][:], in0=STG[a][:], in1=ROW[2][:], op=ALU.add),
                     reads=[("st", qa[c]), ("row", 2)], writes=[("st", qa[c])])
            for c in cs:
                g_ap = prm(l, gname, c)
                b_ap = prm(l, bname, c)
                if final:
                    P.op("act", lambda e, c=c, a=qa[c], g_ap=g_ap, b_ap=b_ap: e.activation(out=Y32[:, c, :], in_=STG[a][:], func=AF.Identity,
                                                                                        scale=g_ap, bias=b_ap),
                         reads=[("st", qa[c]), "y32f"], writes=[("om", 2 * c), ("om", 2 * c + 1)])
                else:
                    P.op("act", lambda e, a=qa[c], d=qd[c], g_ap=g_ap, b_ap=b_ap: e.activation(out=STG[d][:], in_=STG[a][:], func=AF.Identity,
                                                                                            scale=g_ap, bias=b_ap),
                         reads=[("st", qa[c])], writes=[("st", qd[c])])
            if not final:
                for c in cs:
                    P.op("act", lambda e, c=c, d=qd[c]: e.copy(out=XB[:, c, :], in_=STG[d][:]), reads=[("st", qd[c])], writes=[("xb", c)])
                for c in cs:
                    P.op("dve", lambda e, c=c, d=qd[c]: e.tensor_tensor(out=XL[:, c, :], in0=STG[d][:], in1=XB[:, c, :], op=ALU.subtract),
                         reads=[("st", qd[c]), ("xb", c)], writes=[("xl", c)])

    def load_tokens(row0, ntok, dst_hi, dst_lo, hikeys, lokeys, xsbuf):
        P.op("sp", lambda e: e.dma_start(out=XS[xsbuf][0:ntok, :], in_=xin[row0:row0 + ntok, :]),
             reads=["xsf"], writes=[("xs", xsbuf)], dma_key=("xs", xsbuf))
        for c in range(DC):
            q = c % 4
            P.op("pe", lambda e, c=c, q=q: e.transpose(PXT[q][:, 0:ntok], XS[xsbuf][0:ntok, c * 128:(c + 1) * 128],
                                                       IDF[0:ntok, 0:ntok]),
                 reads=[("xs", xsbuf), "idf", "psxy"], writes=[("pxt", q)])
            P.op("act", lambda e, c=c, q=q: e.copy(out=dst_hi(c), in_=PXT[q][:, 0:ntok]),
                 reads=[("pxt", q)], writes=[hikeys(c)])
            if dst_lo is not None:
                P.op("dve", lambda e, c=c, q=q: e.tensor_tensor(out=dst_lo(c), in0=PXT[q][:, 0:ntok], in1=dst_hi(c),
                                                               op=ALU.subtract),
                     reads=[("pxt", q), hikeys(c)], writes=[lokeys(c)])

    blk = [(b * 128, min(128, T - b * 128)) for b in range(NBLK)]

    def load_tile(t):
        fence(OMK + ["arena", "xsf", ("ys", 0), ("ys", 1)] + XBK + XLK + ["psxy"])
        n = 0
        if t == 0:
            load_tokens(0, 128, lambda c: XBH[:, c, :], None, lambda c: "arena_xbh", None, n % 2)
            n += 1
        for (c0, ntok) in blk:
            load_tokens(HALO + t * T + c0, ntok, lambda c, c0=c0, ntok=ntok: XB[:, c, c0:c0 + ntok],
                        lambda c, c0=c0, ntok=ntok: XL[:, c, c0:c0 + ntok],
                        lambda c: ("xb", c), lambda c: ("xl", c), n % 2)
            n += 1

    def attention(t, l):
        first = (t == 0 and l == 0)
        A = ["arena"]
        P.op("dve", lambda e: e.memset(K2[:, :, 672:768], 0.0), reads=A, writes=[("k2", g) for g in range(NKV)])
        P.op("dve", lambda e: e.memset(VF[:, :, 672:768], 0.0), reads=A, writes=[("vf", g) for g in range(KVC)])

        def evac_k(ci, ps, s):
            halves(lambda a, b: P.op("act", lambda e: e.copy(out=K2[:, ci, 128 + a:128 + b], in_=psr(ps, a, b)),
                                     reads=[("ps", s)] + A, writes=[("k2", ci)]))
            if first:
                P.op("act", lambda e: e.copy(out=K2[:, ci, 0:128], in_=ps[:, HF:HF + 128]),
                     reads=[("ps", s)] + A, writes=[("k2", ci)])
            else:
                P.op("dve", lambda e: e.tensor_copy(out=K2[:, ci, 0:128], in_=KH[:, l * NKV + ci, :]),
                     reads=[("kh", l)] + A, writes=[("k2", ci)])
            P.op("dve", lambda e: e.tensor_copy(out=KH[:, l * NKV + ci, :], in_=K2[:, ci, T:T + 128]),
                 reads=[("k2", ci)] + A, writes=[("kh", l)])
        proj(DC, xb_mv, xb_rk, w_in, l, 0, cfg["OFF_K"], NKV, evac_k, halo=(xbh_mv if first else None),
             dup64=True, wcols=KVW)

        def evac_v(ci, ps, s):
            halves(lambda a, b: P.op("act", lambda e: e.copy(out=VF[:, ci, 128 + a:128 + b], in_=psr(ps, a, b)),
                                     reads=[("ps", s)] + A, writes=[("vf", ci)]))
            if first:
                P.op("act", lambda e: e.copy(out=VF[:, ci, 0:128], in_=ps[:, HF:HF + 128]),
                     reads=[("ps", s)] + A, writes=[("vf", ci)])
            else:
                P.op("dve", lambda e: e.tensor_copy(out=VF[:, ci, 0:128], in_=VH[:, l * KVC + ci, :]),
                     reads=[("vh", l)] + A, writes=[("vf", ci)])
            P.op("dve", lambda e: e.tensor_copy(out=VH[:, l * KVC + ci, :], in_=VF[:, ci, T:T + 128]),
                 reads=[("vf", ci)] + A, writes=[("vh", l)])
        proj(DC, xb_mv, xb_rk, w_in, l, 0, cfg["OFF_V"], KVC, evac_v, halo=(xbh_mv if first else None),
             wcols=min(KVW, 128 * KVC))
        mw = KVW if KVW < 128 else 128
        for i in range(6):
            for ch in range(KVC):
                P.op("pe", lambda e, i=i, ch=ch: e.transpose(psYb[0:128, 1536 + ch * 128:1536 + ch * 128 + mw],
                                                             VF[0:mw, ch, i * 128:(i + 1) * 128], IDB[0:mw, 0:mw]),
                     reads=[("vf", ch), "idb", "psxy"] + A, writes=["pvt"])
            P.op("dve", lambda e, i=i: e.tensor_copy(out=VT[:, i, :], in_=psYb[:, 1536:1536 + KVW]),
                 reads=["pvt"] + A, writes=[("vt", i)])

        nu = [0]

        def evac_q(cq):
            def f(ci, ps, s):
                halves(lambda a, b: P.op("act", lambda e: e.copy(out=QB[:, cq % 2, a:b], in_=psr(ps, a, b)),
                                         reads=[("ps", s)] + A, writes=[("qb", cq % 2)]))
            return f

        def unit_s(cq, b):
            j = nu[0] % 2
            nq = blk[b][1]
            g = (2 * cq) // cfg["GROUP"]
            maskoff = 256 if (t == 0 and b == 1) else 0
            for p in range(2):
                P.op("pe", lambda e, p=p: e.matmul(psX[0:nq, p * 512:p * 512 + 256],
                                                   lhsT=QB[p * 64:(p + 1) * 64, cq % 2, b * 128:b * 128 + nq],
                                                   rhs=K2[p * 64:(p + 1) * 64, g, b * 128:b * 128 + 256], start=True, stop=True),
                     reads=[("qb", cq % 2), ("k2", g), "psxy"] + A, writes=[("pss", j, p)])
            for p in range(2):
                P.op("dve", lambda e, p=p: e.scalar_tensor_tensor(out=SS[0:nq, j, p, :], in0=psX[0:nq, p * 512:p * 512 + 256],
                                                                 scalar=0.125, in1=MSK[0:nq, maskoff:maskoff + 256],
                                                                 op0=ALU.mult, op1=ALU.add),
                     reads=[("pss", j, p), "msk"] + A, writes=[("ss", j, p)])
            for p in range(2):
                P.op("act", lambda e, p=p: e.activation(out=SS[0:nq, j, p, :], in_=SS[0:nq, j, p, :], func=AF.Exp,
                                                        accum_out=SM[0:nq, j * 2 + p:j * 2 + p + 1]),
                     reads=[("ss", j, p)] + A, writes=[("ss", j, p), ("sm", j, p)])
            h0 = 2 * cq
            P.op("dve", lambda e: e.tensor_tensor(out=SM[0:nq, 4 + j * 2:4 + j * 2 + 2], in0=SM[0:nq, j * 2:j * 2 + 2],
                                                  in1=ESK[0:nq, l * NQH + h0:l * NQH + h0 + 2], op=ALU.add),
                 reads=[("sm", j, 0), ("sm", j, 1), "esk"], writes=[("sd", j)])
            P.op("dve", lambda e: e.reciprocal(out=SM[0:nq, 8 + j * 2:8 + j * 2 + 2], in_=SM[0:nq, 4 + j * 2:4 + j * 2 + 2]),
                 reads=[("sd", j)], writes=[("sr", j)])
            for p in range(2):
                P.op("dve", lambda e, p=p: e.tensor_scalar(out=PN[0:nq, j, p, :], in0=SS[0:nq, j, p, :],
                                                          scalar1=SM[0:nq, 8 + j * 2 + p:8 + j * 2 + p + 1], scalar2=None, op0=ALU.mult),
                     reads=[("ss", j, p), ("sr", j)] + A, writes=[("pn", j, p)])
            nu[0] += 1
            return (cq, b, j, nq, g)

        def unit_o(u):
            cq, b, j, nq, g = u
            for p in range(2):
                for kb in range(2):
                    o0 = (p * 2 + kb) * 128
                    P.op("pe", lambda e, p=p, kb=kb, o0=o0: e.transpose(psYb[0:128, o0:o0 + nq], PN[0:nq, j, p, kb * 128:(kb + 1) * 128],
                                                                        IDB[0:nq, 0:nq]),
                         reads=[("pn", j, p), "idb", "psxy"] + A, writes=[("ppt", j)])
            src = psYb[:, 0:512].rearrange("p (a b) -> p a b", b=128)[:, :, 0:nq]
            dst = PTS[:, j, :].rearrange("p (a b) -> p a b", b=128)[:, :, 0:nq]
            P.op("act", lambda e: e.copy(out=dst, in_=src), reads=[("ppt", j)] + A, writes=[("pt", j)])
            for p in range(2):
                for kb in range(2):
                    P.op("pe", lambda e, p=p, kb=kb: e.matmul(psY[p * 64:(p + 1) * 64, 512:512 + nq],
                                                              lhsT=VT[:, b + kb, g * 64:(g + 1) * 64],
                                                              rhs=PTS[:, j, (p * 2 + kb) * 128:(p * 2 + kb) * 128 + nq],
                                                              start=(kb == 0), stop=(kb == 1)),
                         reads=[("vt", b + kb), ("pt", j), "psxy"] + A, writes=[("pso", j)])
            P.op("act", lambda e: e.copy(out=OM[:, cq, b * 128:b * 128 + nq], in_=psY[:, 512:512 + nq]),
                 reads=[("pso", j)], writes=[("om", cq)])

        pend = [None]

        def unit_cb(cq, b):
            def f():
                u = unit_s(cq, b)
                if pend[0] is not None:
                    unit_o(pend[0])
                pend[0] = u
            return f
        proj(DC, xb_mv, xb_rk, w_in, l, 0, 0, 1, evac_q(0), wcols=128)
        for cq in range(AC):
            cbs = [unit_cb(cq, b) for b in range(NBLK)]
            if cq + 1 < AC:
                proj(DC, xb_mv, xb_rk, w_in, l, 0, (cq + 1) * 128, 1, evac_q(cq + 1), wcols=128, inter=cbs)
            else:
                for f in cbs:
                    f()
        unit_o(pend[0])

    def conf_branch(t, l):
        first = (t == 0 and l == 0)
        zflag = (t == 0 and l == 1)
        A = ["arena"]

        def conf_chunk(i):
            qs = stg()

            def evac_gate(ci, ps, s):
                halves(lambda a, b: P.op("act", lambda e: e.activation(out=STG[qs][:, a:b], in_=psr(ps, a, b), func=AF.Sigmoid),
                                         reads=[("ps", s)], writes=[("st", qs)]))
                if first:
                    P.op("act", lambda e: e.activation(out=YB[:, 0:30], in_=ps[:, HF + 98:HF + 128], func=AF.Sigmoid),
                         reads=[("ps", s)], writes=["yb"])

            def evac_val(ci, ps, s):
                halves(lambda a, b: P.op("dve", lambda e: e.tensor_tensor(out=CB[:, 30 + a:30 + b], in0=psr(ps, a, b), in1=STG[qs][:, a:b], op=ALU.mult),
                                         reads=[("ps", s), ("st", qs)], writes=["cb"]))
                if first:
                    P.op("dve", lambda e: e.tensor_tensor(out=CB[:, 0:30], in0=ps[:, HF + 98:HF + 128], in1=YB[:, 0:30], op=ALU.mult),
                         reads=[("ps", s), "yb"], writes=["cb"])
                else:
                    P.op("dve", lambda e: e.tensor_copy(out=CB[:, 0:30], in_=CH[:, l * CC + i, :]), reads=[("ch", l, i)], writes=["cb"])
                if zflag:
                    P.op("dve", lambda e: e.tensor_scalar(out=CB[:, 30:30 + 128], in0=CB[:, 30:30 + 128], scalar1=PRM[:, cfg["NP"] - 1:cfg["NP"]],
                                                         scalar2=None, op0=ALU.mult),
                         reads=["cb", "prm"], writes=["cb"])
                P.op("dve", lambda e: e.tensor_copy(out=CH[:, l * CC + i, :], in_=CB[:, T:T + 30]), reads=["cb"], writes=[("ch", l, i)])
            proj(DC, xb_mv, xb_rk, w_in, l, 0, cfg["OFF_CONF"] + cfg["CONF_W"] + i * 128, 1, evac_gate,
                 halo=(xbh_mv if first else None), wcols=128)
            if pend_conv[0] is not None:
                pend_conv[0]()
            proj(DC, xb_mv, xb_rk, w_in, l, 0, cfg["OFF_CONF"] + i * 128, 1, evac_val,
                 halo=(xbh_mv if first else None), wcols=128)

            def conv():
                s2 = ps_ctr[0] % 2
                ps_ctr[0] += 1
                ps2 = PSL[s2]
                for j in range(CONF_K):
                    d = dg_ctr[0] % NDG
                    dg_ctr[0] += 1
                    P.op("dve", lambda e, d=d, j=j: e.tensor_scalar(out=DG[d][:], in0=IDB[:], scalar1=prm(l, "cw", i * CONF_K + j),
                                                                   scalar2=None, op0=ALU.mult),
                         reads=["idb", "prm"], writes=[("dg", d)])

                    def emit(e, d=d, j=j):
                        ins = None
                        for h in range(2):
                            ins = e.matmul(ps2[:, h * 512:h * 512 + HF], lhsT=DG[d][:], rhs=CB[:, j + h * HF:j + h * HF + HF],
                                           start=(j == 0), stop=(j == CONF_K - 1))
                        return ins
                    P.op("pe", emit, reads=[("dg", d), "cb"], writes=[("ps", s2)])
                halves(lambda a, b: P.op("act", lambda e: e.activation(out=YC[:, i, a:b], in_=psr(ps2, a, b), func=AF.Identity,
                                                                      bias=prm(l, "cb", i)),
                                         reads=[("ps", s2), "prm"] + A, writes=[("yc", i)]))
                stats_acc(lambda: YC[:, i, :], ("yc", i))
            pend_conv[0] = conv
        pend_conv = [None]
        for i in range(CC):
            conf_chunk(i)
        pend_conv[0]()
        ln_rows(ONEC)
        for i0 in range(0, CC, 4):
            cs = list(range(i0, min(CC, i0 + 4)))
            qa = {i: stg() for i in cs}
            for i in cs:
                P.op("dve", lambda e, i=i, a=qa[i]: e.tensor_tensor(out=STG[a][:], in0=YC[:, i, :], in1=ROW[1][:], op=ALU.mult),
                     reads=[("yc", i), ("row", 1)] + A, writes=[("st", qa[i])])
            for i in cs:
                P.op("dve", lambda e, a=qa[i]: e.tensor_tensor(out=STG[a][:], in0=STG[a][:], in1=ROW[2][:], op=ALU.add),
                     reads=[("st", qa[i]), ("row", 2)], writes=[("st", qa[i])])
            for i in cs:
                P.op("act", lambda e, i=i, a=qa[i]: e.activation(out=OM[:, AC + i, :], in_=STG[a][:], func=AF.Silu,
                                                                scale=prm(l, "cg", i), bias=prm(l, "cbb", i)),
                     reads=[("st", qa[i]), "prm"], writes=[("om", AC + i)])

    def sc_branch(t, l):
        first = (t == 0 and l == 0)
        zflag = (t == 0 and l == 1)
        base = cfg["OFF_SC"]
        SW = cfg["SC_W"]

        def sc_chunk(i):
            qx = stg()

            def evac_x(ci, ps, s):
                halves(lambda a, b: P.op("act", lambda e: e.copy(out=STG[qx][:, a:b], in_=psr(ps, a, b)),
                                         reads=[("ps", s)], writes=[("st", qx)]))
                if first:
                    P.op("act", lambda e: e.copy(out=YB[:, 0:2], in_=ps[:, HF + 126:HF + 128]), reads=[("ps", s)], writes=["yb"])

            def evac_c(ci, ps, s):
                halves(lambda a, b: P.op("dve", lambda e: e.tensor_tensor(out=UB[:, 2 + a:2 + b], in0=psr(ps, a, b), in1=STG[qx][:, a:b], op=ALU.mult),
                                         reads=[("ps", s), ("st", qx)], writes=["ub"]))
                if first:
                    P.op("dve", lambda e: e.tensor_tensor(out=UB[:, 0:2], in0=ps[:, HF + 126:HF + 128], in1=YB[:, 0:2], op=ALU.mult),
                         reads=[("ps", s), "yb"], writes=["ub"])
                else:
                    P.op("dve", lambda e: e.tensor_copy(out=UB[:, 0:2], in_=UH[:, l * SCC + i, :]), reads=[("uh", l, i)], writes=["ub"])
                if zflag:
                    P.op("dve", lambda e: e.tensor_scalar(out=UB[:, 2:2 + 128], in0=UB[:, 2:2 + 128], scalar1=PRM[:, cfg["NP"] - 1:cfg["NP"]],
                                                         scalar2=None, op0=ALU.mult),
                         reads=["ub", "prm"], writes=["ub"])
                P.op("dve", lambda e: e.tensor_copy(out=UH[:, l * SCC + i, :], in_=UB[:, T:T + 2]), reads=["ub"], writes=[("uh", l, i)])
                qy = stg()
                P.op("dve", lambda e: e.tensor_scalar(out=STG[qy][:], in0=UB[:, 0:T], scalar1=prm(l, "sw", i * SC_K), scalar2=None, op0=ALU.mult),
                     reads=["ub", "prm"], writes=[("st", qy)])
                for j in range(1, SC_K):
                    P.op("dve", lambda e, j=j: e.scalar_tensor_tensor(out=STG[qy][:], in0=UB[:, j:j + T], scalar=prm(l, "sw", i * SC_K + j),
                                                                     in1=STG[qy][:], op0=ALU.mult, op1=ALU.add),
                         reads=["ub", ("st", qy), "prm"], writes=[("st", qy)])
                evac_c.qy = qy

            def evac_b(ci, ps, s):
                qy = evac_c.qy
                halves(lambda a, b: P.op("dve", lambda e: e.tensor_tensor(out=OM[:, AC + CC + i, a:b], in0=psr(ps, a, b), in1=STG[qy][:, a:b], op=ALU.mult),
                                         reads=[("ps", s), ("st", qy)], writes=[("om", AC + CC + i)]))
            proj(DC, xb_mv, xb_rk, w_in, l, 0, base + 2 * SW + i * 128, 1, evac_x, halo=(xbh_mv if first else None), wcols=128)
            proj(DC, xb_mv, xb_rk, w_in, l, 0, base + SW + i * 128, 1, evac_c, halo=(xbh_mv if first else None), wcols=128)
            proj(DC, xb_mv, xb_rk, w_in, l, 0, base + i * 128, 1, evac_b, wcols=128)
        for i in range(SCC):
            sc_chunk(i)

    def merge_phase(t, l):
        ob_mv = lambda off: (lambda kc, h: OM[:, off + kc, h * HF:(h + 1) * HF])
        ob_rk = lambda off: (lambda kc: [("om", off + kc)])
        branches = ((w_pa, AC, 0), (w_pb, CC, AC), (w_pc, SCC, AC + CC))
        def merge_pair(c0):
            sg = [stg(), stg()]
            macc = [stg(), stg()]
            for bi, (wp, nkb, off) in enumerate(branches):
                def evac_g(ci, ps, s):
                    halves(lambda a, b: P.op("act", lambda e: e.activation(out=STG[sg[ci]][:, a:b], in_=psr(ps, a, b), func=AF.Sigmoid),
                                             reads=[("ps", s)], writes=[("st", sg[ci])]))

                def evac_p(ci, ps, s, bi=bi):
                    c = c0 + ci
                    if bi == 0:
                        halves(lambda a, b: P.op("dve", lambda e: e.tensor_tensor(out=STG[macc[ci]][:, a:b], in0=psr(ps, a, b), in1=STG[sg[ci]][:, a:b], op=ALU.mult),
                                                 reads=[("ps", s), ("st", sg[ci])], writes=[("st", macc[ci])]))
                    else:
                        halves(lambda a, b: P.op("dve", lambda e: e.tensor_tensor(out=STG[sg[ci]][:, a:b], in0=psr(ps, a, b), in1=STG[sg[ci]][:, a:b], op=ALU.mult),
                                                 reads=[("ps", s), ("st", sg[ci])], writes=[("st", sg[ci])]))
                        if bi == 1:
                            P.op("dve", lambda e: e.tensor_tensor(out=STG[macc[ci]][:], in0=STG[macc[ci]][:], in1=STG[sg[ci]][:], op=ALU.add),
                                 reads=[("st", macc[ci]), ("st", sg[ci])], writes=[("st", macc[ci])])
                        else:
                            P.op("dve", lambda e: e.tensor_tensor(out=OM[:, DC + c, :], in0=STG[macc[ci]][:], in1=STG[sg[ci]][:], op=ALU.add),
                                 reads=[("st", macc[ci]), ("st", sg[ci])], writes=[("om", DC + c)])
                proj(DC, xb_mv, xb_rk, w_in, l, 0, cfg["OFF_G"] + bi * D + c0 * 128, 2, evac_g)
                proj(nkb, ob_mv(off), ob_rk(off), wp, l, 0, c0 * 128, 2, evac_p)
        for c0 in range(0, DC, 2):
            merge_pair(c0)

    def wout_phase(t, l):
        mb_mv = lambda kc, h: OM[:, DC + kc, h * HF:(h + 1) * HF]
        mb_rk = lambda kc: [("om", DC + kc)]
        for c0 in range(0, DC, 2):
            proj(DC, mb_mv, mb_rk, w_out, l, 0, c0 * 128, 2,
                 lambda ci, ps, s: stream_update(c0 + ci, ps, s, ALPHA, True))
        ln_rows(ONED)
        ln_stream(l, "ln1g", "ln1b", False)

    def ffn_update(c, ps, s, g):
        XE = lambda c: OM[:, FGMAX + c, :]
        xek = ("om", FGMAX + c)
        qa = stg()
        last = (g == G - 1)
        if g == 0:
            for h in range(2):
                P.op("dve", lambda e, h=h: e.scalar_tensor_tensor(out=STG[qa][:, h * HF:(h + 1) * HF], in0=XL[:, c, h * HF:(h + 1) * HF],
                                                                 scalar=float(ALPHA), in1=ps[:, h * 512:h * 512 + HF],
                                                                 op0=ALU.mult, op1=ALU.add),
                     reads=[("xl", c), ("ps", s)], writes=[("st", qa)])
        else:
            for h in range(2):
                P.op("dve", lambda e, h=h: e.tensor_tensor(out=STG[qa][:, h * HF:(h + 1) * HF], in0=ps[:, h * 512:h * 512 + HF],
                                                          in1=XL[:, c, h * HF:(h + 1) * HF], op=ALU.add),
                     reads=[("xl", c), ("ps", s)], writes=[("st", qa)])
            P.op("dve", lambda e: e.tensor_tensor(out=STG[qa][:], in0=STG[qa][:], in1=XE(c), op=ALU.add),
                 reads=[("st", qa), xek], writes=[("st", qa)])
        if not last:
            P.op("act", lambda e: e.copy(out=XL[:, c, :], in_=STG[qa][:]), reads=[("st", qa)], writes=[("xl", c)])
            P.op("dve", lambda e: e.tensor_tensor(out=XE(c), in0=STG[qa][:], in1=XL[:, c, :], op=ALU.subtract),
                 reads=[("st", qa), ("xl", c)], writes=[xek])
        else:
            qb = stg()
            P.op("dve", lambda e: e.scalar_tensor_tensor(out=STG[qb][:], in0=XB[:, c, :], scalar=float(ALPHA), in1=STG[qa][:],
                                                        op0=ALU.mult, op1=ALU.add),
                 reads=[("xb", c), ("st", qa)], writes=[("st", qb)])
            stats_acc(lambda: STG[qb][:], ("st", qb))
            P.op("act", lambda e: e.copy(out=XB[:, c, :], in_=STG[qb][:]), reads=[("st", qb)], writes=[("xb", c)])
            P.op("dve", lambda e: e.tensor_tensor(out=XL[:, c, :], in0=STG[qb][:], in1=XB[:, c, :], op=ALU.subtract),
                 reads=[("st", qb), ("xb", c)], writes=[("xl", c)])

    def ffn_phase(t, l):
        act_mv = lambda kc, h: OM[:, kc, h * HF:(h + 1) * HF]
        act_rk = lambda kc: [("om", kc)]
        final = (l == L - 1)

        def ffn_in_block(g, j0, nch, gbase):
            sl = [stg(), stg()]

            def evac_gate(ci, ps, s):
                halves(lambda a, b: P.op("act", lambda e: e.activation(out=STG[sl[ci]][:, a:b], in_=psr(ps, a, b), func=AF.Silu),
                                         reads=[("ps", s)], writes=[("st", sl[ci])]))

            def evac_up(ci, ps, s):
                halves(lambda a, b: P.op("dve", lambda e: e.tensor_tensor(out=OM[:, j0 + ci, a:b], in0=psr(ps, a, b), in1=STG[sl[ci]][:, a:b], op=ALU.mult),
                                         reads=[("ps", s), ("st", sl[ci])], writes=[("om", j0 + ci)]))
            col = (gbase + j0) * 128
            proj(DC, xb_mv, xb_rk, w_fi, l, 0, col, nch, evac_gate)
            proj(DC, xb_mv, xb_rk, w_fi, l, 0, DFF + col, nch, evac_up)

        def ffn_down_pair(g, c0, gbase, fg):
            proj(fg, act_mv, act_rk, w_fd, l, gbase * 128, c0 * 128, 2,
                 lambda ci, ps, s: ffn_update(c0 + ci, ps, s, g))
        gbase = 0
        for g in range(G):
            fg = FGS[g]
            j0 = 0
            while j0 < fg:
                nch = min(2, fg - j0)
                ffn_in_block(g, j0, nch, gbase)
                j0 += nch
            for c0 in range(0, DC, 2):
                ffn_down_pair(g, c0, gbase, fg)
            gbase += fg
        ln_rows(ONED)
        if final:
            fence(OMK + ["y32f"])
        ln_stream(l, "ln2g", "ln2b", final)

    def store_tile(t):
        fence(XBK + XLK + [("ys", 0), ("ys", 1), "psxy"])
        n = 0
        for bi, (c0, ntok) in enumerate(blk):
            if t == 0 and bi == 0:
                continue
            yb = n % 2
            for c in range(DC):
                q = c % 4
                P.op("pe", lambda e, c=c, q=q, c0=c0, ntok=ntok: e.transpose(PXT[q][0:ntok, :], Y32[:, c, c0:c0 + ntok], IDF[:, :]),
                     reads=[("om", 2 * c), ("om", 2 * c + 1), "idf", "psxy"], writes=[("pxt", q)])
                if c % 2 == 0:
                    P.op("act", lambda e, c=c, q=q, yb=yb, ntok=ntok: e.copy(out=YS[yb][0:ntok, c * 128:(c + 1) * 128], in_=PXT[q][0:ntok, :]),
                         reads=[("pxt", q)], writes=[("ys", yb)])
                else:
                    P.op("dve", lambda e, c=c, q=q, yb=yb, ntok=ntok: e.tensor_copy(out=YS[yb][0:ntok, c * 128:(c + 1) * 128], in_=PXT[q][0:ntok, :]),
                         reads=[("pxt", q)], writes=[("ys", yb)])
            r0 = t * T + c0 - HALO
            P.op("sp", lambda e, r0=r0, yb=yb, ntok=ntok: e.dma_start(out=y_d[r0:r0 + ntok, :], in_=YS[yb][0:ntok, :]),
                 reads=[("ys", yb)], dma_key=("yo", yb))
            n += 1

    EPSC = sb("EPSC", [128, 1], F32)

    def prologue():
        P.op("sp", lambda e: e.dma_start(out=PRM[:], in_=prm_d), writes=["prm"], dma_key="c0")
        P.op("sp", lambda e: e.dma_start(out=MSK[:], in_=msk_d), writes=["msk"], dma_key="c1")
        P.op("sp", lambda e: e.dma_start(out=IDF[:], in_=idf_d), writes=["idf"], dma_key="c2")
        P.op("dve", lambda e: e.tensor_copy(out=IDB[:], in_=IDF[:]), reads=["idf"], writes=["idb"])
        P.op("dve", lambda e: e.memset(ONED[:], 1.0 / D), writes=["ones"])
        P.op("dve", lambda e: e.memset(ONEC[:], 1.0 / cfg["CONF_W"]), writes=["ones"])
        P.op("dve", lambda e: e.memset(EPSC[:], LN_EPS), writes=["eps"])
        P.op("dve", lambda e: e.memset(CH[:], 0.0), writes=[("ch", l, i) for l in range(L) for i in range(CC)])
        P.op("dve", lambda e: e.memset(UH[:], 0.0), writes=[("uh", l, i) for l in range(L) for i in range(SCC)])
        P.op("dve", lambda e: e.memset(KH[:], 0.0), writes=[("kh", l) for l in range(L)])
        P.op("dve", lambda e: e.memset(VH[:], 0.0), writes=[("vh", l) for l in range(L)])
        for l in range(L):
            o, n = PL["sink"]
            P.op("act", lambda e, l=l, o=o, n=n: e.activation(out=ESK[:, l * NQH:(l + 1) * NQH], in_=PRM[:, l * PLN + o:l * PLN + o + n], func=AF.Exp),
                 reads=["prm"], writes=["esk"])

    def body():
        for t in range(NT):
            load_tile(t)
            fence(OMK + [("xs", 0), ("xs", 1)])
            for l in range(L):
                fence(OMK[DC:] + ["arena", "psxy"])
                attention(t, l)
                fence(["arena", "psxy"])
                conf_branch(t, l)
                sc_branch(t, l)
                fence(OMK[DC:] + ["arena"])
                merge_phase(t, l)
                fence(["psxy"])
                wout_phase(t, l)
                fence(["psxy"])
                ffn_phase(t, l)
            store_tile(t)

    P.dry = True
    body()
    P.dry = False
    stg_ctr[0] = 0
    ps_ctr[0] = 0
    acc_first[0] = True
    prologue()
    body()
    P.wait_all_dma("sp", [("yo", 0), ("yo", 1)])
    info = P.emit()
    st.close()
    return nc, info


def pack_params(cfg, inp):
    L, DC, CC, SCC, NQH = cfg["L"], cfg["DC"], cfg["CC"], cfg["SCC"], cfg["NQH"]
    PL, PLN = cfg["PL"], cfg["PLN"]
    out = np.zeros((128, cfg["NP"]), np.float32)

    def chunked(v):
        return np.ascontiguousarray(np.asarray(v, np.float32).reshape(-1, 128).T)
    for l in range(L):
        b = l * PLN
        for name, key in (("ln1g", "ln1_g"), ("ln1b", "ln1_b"), ("ln2g", "ln2_g"), ("ln2b", "ln2_b"),
                          ("cb", "conf_dw_b"), ("cg", "conf_ln_g"), ("cbb", "conf_ln_b")):
            o, n = PL[name]
            out[:, b + o:b + o + n] = chunked(inp[key][l])
        o, n = PL["cw"]
        cw = np.asarray(inp["conf_dw"][l], np.float32)
        out[:, b + o:b + o + n] = cw.T.reshape(CC, 128, CONF_K).transpose(1, 0, 2).reshape(128, CC * CONF_K)
        o, n = PL["sw"]
        sw = np.asarray(inp["sc_dw"][l], np.float32)
        out[:, b + o:b + o + n] = sw.T.reshape(SCC, 128, SC_K).transpose(1, 0, 2).reshape(128, SCC * SC_K)
        o, n = PL["sink"]
        out[:, b + o:b + o + n] = np.broadcast_to(np.asarray(inp["attn_sinks"][l], np.float32)[None, :], (128, NQH))
    return out


def make_masks(seq_start):
    qi = np.arange(128)[:, None]
    kj = np.arange(256)[None, :]
    rel = qi + 128 - kj
    band = (rel >= 0) & (rel < 128)
    mA = np.where(band, 0.0, NEG).astype(np.float32)
    mB = mA.copy()
    if seq_start:
        mB[:, 0:128] = NEG
    return np.concatenate([mA, mB], axis=1)


_CACHE = {}


def run(cfg, inp):
    key = tuple(sorted((k, v) for k, v in cfg.items() if not isinstance(v, (dict, list))))
    if key not in _CACHE:
        _CACHE[key] = build_program(cfg)
    nc, info = _CACHE[key]
    D, OWN, NT = cfg["D"], cfg["OWN"], cfg["NT"]
    x = np.asarray(inp["x"], np.float32)
    Bn, S = x.shape[0], x.shape[1]
    ncs = cfg["NCORE_SEQ"]
    prm_base = pack_params(cfg, inp)
    idf = np.eye(128, dtype=np.float32)
    wmap = {"w_in": "w_in", "w_pa": "w_proj_attn", "w_pb": "w_proj_conf", "w_pc": "w_proj_sc",
            "w_out": "w_out", "w_fi": "w_ffn_in", "w_fd": "w_ffn_down"}
    weights = {k: np.ascontiguousarray(np.asarray(inp[v], np.float32)) for k, v in wmap.items()}
    in_maps = []
    for core in range(8):
        b, j = divmod(core, ncs)
        s0 = j * OWN
        xin = np.zeros((HALO + NT * T, D), np.float32)
        lo = s0 - 2 * HALO
        src_lo = max(lo, 0)
        xin[src_lo - lo:, :] = x[b, src_lo:s0 + OWN, :]
        p = prm_base.copy()
        p[:, -1] = 0.0 if j == 0 else 1.0
        m = {"xin": xin, "prm": p, "msk": make_masks(j == 0), "idf": idf}
        m.update(weights)
        in_maps.append(m)
    res = run_bass_kernel_spmd(nc, in_maps, core_ids=list(range(8)))
    out = np.zeros((Bn, S, D), np.float32)
    for core in range(8):
        b, j = divmod(core, ncs)
        out[b, j * OWN:(j + 1) * OWN, :] = res.results[core]["y"]
    return out


CFG = None


def kernel(**inputs):
    global CFG
    if CFG is None:
        CFG = make_cfg()
    return run(CFG, inputs)
```
